# Optimizing a Trainium2 kernel written in Bass

```python
import jax, jax.numpy as jnp
from jax import lax
import numpy as np

D_MODEL = 1024
BATCH = 8
SEQ = 2048
DEPTH = 2

N_HEADS = 16
HEAD_DIM = D_MODEL // N_HEADS
N_MIXERS = 2
GRID_W = 64
NA_ROWS = 8
NA_COLS = 16
DIL_GROUPS = ((128, 1), (512, 4), (2048, 16))
N_GROUPS = len(DIL_GROUPS)
BAND_BLOCK = 128
D_FF = -(-8 * D_MODEL // (3 * 256)) * 256
RMS_EPS = 1e-6
NEG_INF = -1e30
N_A_LAYERS = (DEPTH + 1) // 2
N_B_LAYERS = DEPTH // 2

kernel_name = "hybrid_natten_dilated_encoder"


def rms_norm(x, g):
    xf = x.astype(jnp.float32)
    y = xf * lax.rsqrt(jnp.mean(xf * xf, axis=-1, keepdims=True) + RMS_EPS)
    return (y * g.astype(jnp.float32)).astype(x.dtype)


def alibi_slopes(n):
    return 2.0 ** (-8.0 * jnp.arange(1, n + 1, dtype=jnp.float32) / n)


def swiglu(x, w_gate, w_up, w_down):
    return (jax.nn.silu(x @ w_gate) * (x @ w_up)) @ w_down


def neighbourhood_attention(x, w_qkv, w_o, rpb):
    b, s, _ = x.shape
    rows = s // GRID_W
    kh = min(NA_ROWS, rows)
    qkv = (x @ w_qkv).reshape(b, rows, GRID_W, 3, N_HEADS, HEAD_DIM)
    q, k, v = (jnp.transpose(qkv[:, :, :, i], (0, 3, 1, 2, 4)) for i in range(3))
    q = q * HEAD_DIM ** -0.5
    col = jnp.arange(GRID_W)
    col_start = jnp.clip(col - NA_COLS // 2, 0, GRID_W - NA_COLS)
    col_mask = (col[None, :] >= col_start[:, None]) & (col[None, :] < col_start[:, None] + NA_COLS)
    col_idx = jnp.clip(col[None, :] - col[:, None] + NA_COLS - 1, 0, 2 * NA_COLS - 2)
    rpb_cols = rpb.astype(jnp.float32)[:, :, col_idx]

    def row_block(i):
        rs = jnp.clip(i - kh // 2, 0, rows - kh)
        qi = lax.dynamic_index_in_dim(q, i, axis=2, keepdims=False)
        kr = lax.dynamic_slice_in_dim(k, rs, kh, axis=2)
        vr = lax.dynamic_slice_in_dim(v, rs, kh, axis=2)
        bias = lax.dynamic_slice_in_dim(rpb_cols, rs - i + NA_ROWS - 1, kh, axis=1)
        sc = jnp.einsum('bhqd,bhrkd->bhqrk', qi, kr).astype(jnp.float32)
        sc = sc + jnp.transpose(bias, (0, 2, 1, 3))[None]
        sc = jnp.where(col_mask[:, None, :], sc, NEG_INF)
        p = jax.nn.softmax(sc.reshape(b, N_HEADS, GRID_W, kh * GRID_W), axis=-1).reshape(sc.shape)
        return jnp.einsum('bhqrk,bhrkd->bhqd', p.astype(vr.dtype), vr)

    o = lax.map(row_block, jnp.arange(rows))
    o = jnp.transpose(o, (1, 0, 3, 2, 4)).reshape(b, s, D_MODEL)
    return o @ w_o


def banded_attention(q, k, v, radius, slope_dist):
    n, h, l, dh = q.shape
    nb = -(-l // BAND_BLOCK)
    lp = nb * BAND_BLOCK
    qb = jnp.pad(q, ((0, 0), (0, 0), (0, lp - l), (0, 0))).reshape(n, h, nb, BAND_BLOCK, dh)

    def key_blocks(t):
        tp = jnp.pad(t, ((0, 0), (0, 0), (radius, lp - l + BAND_BLOCK - radius), (0, 0)))
        tp = tp.reshape(n, h, nb + 1, BAND_BLOCK, dh)
        return jnp.concatenate([tp[:, :, :-1], tp[:, :, 1:]], axis=3)

    kb, vb = key_blocks(k), key_blocks(v)
    qi = jnp.arange(lp).reshape(nb, BAND_BLOCK)
    kj = jnp.arange(nb)[:, None] * BAND_BLOCK - radius + jnp.arange(2 * BAND_BLOCK)[None, :]
    dist = jnp.abs(qi[:, :, None] - kj[:, None, :])
    valid = (dist <= radius) & (kj[:, None, :] >= 0) & (kj[:, None, :] < l)
    sc = jnp.einsum('nhbqd,nhbkd->nhbqk', qb, kb).astype(jnp.float32)
    sc = sc - slope_dist[None, :, None, None, None] * dist.astype(jnp.float32)
    sc = jnp.where(valid, sc, NEG_INF)
    lse = jax.nn.logsumexp(sc, axis=-1)
    p = jnp.exp(sc - lse[..., None])
    o = jnp.einsum('nhbqk,nhbkd->nhbqd', p.astype(vb.dtype), vb)
    return o.reshape(n, h, lp, dh)[:, :, :l], lse.reshape(n, h, lp)[:, :, :l]


def dilated_attention(x, w_qkv, w_o):
    b, s, _ = x.shape
    qkv = (x @ w_qkv).reshape(b, s, N_GROUPS, 3, N_HEADS, HEAD_DIM)
    slopes = alibi_slopes(N_HEADS)
    outs, lses = [], []
    for g, (window, dil) in enumerate(DIL_GROUPS):
        radius = window // (2 * dil)
        l = s // dil

        def to_sub(t):
            t = jnp.transpose(t.reshape(b, l, dil, N_HEADS, HEAD_DIM), (0, 2, 3, 1, 4))
            return t.reshape(b * dil, N_HEADS, l, HEAD_DIM)

        q, k, v = (to_sub(qkv[:, :, g, i]) for i in range(3))
        o, lse = banded_attention(q * HEAD_DIM ** -0.5, k, v, radius, slopes * dil)
        o = jnp.transpose(o.reshape(b, dil, N_HEADS, l, HEAD_DIM), (0, 3, 1, 2, 4)).reshape(b, s, N_HEADS, HEAD_DIM)
        lse = jnp.transpose(lse.reshape(b, dil, N_HEADS, l), (0, 3, 1, 2)).reshape(b, s, N_HEADS)
        outs.append(o)
        lses.append(lse)
    alpha = jax.nn.softmax(jnp.stack(lses, axis=0), axis=0)
    o = jnp.einsum('gbsh,gbshd->bshd', alpha, jnp.stack(outs, axis=0).astype(jnp.float32)).astype(x.dtype)
    return o.reshape(b, s, D_MODEL) @ w_o


def setup_inputs(seed: int = 0) -> dict:
    key = jax.random.key(seed)
    ks = jax.random.split(key, 14)
    d = D_MODEL

    def w(k, shape, fan_in):
        return jax.random.normal(k, shape, jnp.float32) * fan_in ** -0.5

    def gain(k):
        return 1.0 + 0.02 * jax.random.normal(k, (DEPTH, d), jnp.float32)

    return {
        "x": jax.random.normal(ks[0], (BATCH, SEQ, d), jnp.float32),
        "norm_mix_pre": gain(ks[1]),
        "norm_mix_post": gain(ks[2]),
        "norm_ffn_pre": gain(ks[3]),
        "norm_ffn_post": gain(ks[4]),
        "na_w_qkv": w(ks[5], (N_A_LAYERS, d, 3 * d), d),
        "na_w_o": w(ks[6], (N_A_LAYERS, d, d), d),
        "na_rpb": 0.5 * jax.random.normal(ks[7], (N_A_LAYERS, N_HEADS, 2 * NA_ROWS - 1, 2 * NA_COLS - 1), jnp.float32),
        "dil_w_qkv": w(ks[8], (N_B_LAYERS, d, N_GROUPS * 3 * d), d),
        "dil_w_o": w(ks[9], (N_B_LAYERS, d, d), d),
        "ffn_w_gate": w(ks[10], (DEPTH, d, D_FF), d),
        "ffn_w_up": w(ks[11], (DEPTH, d, D_FF), d),
        "ffn_w_down": w(ks[12], (DEPTH, D_FF, d), D_FF),
    }


def reference(x, norm_mix_pre, norm_mix_post, norm_ffn_pre, norm_ffn_post, na_w_qkv, na_w_o, na_rpb,
              dil_w_qkv, dil_w_o, ffn_w_gate, ffn_w_up, ffn_w_down):
    for layer in range(DEPTH):
        j = layer // N_MIXERS
        h = rms_norm(x, norm_mix_pre[layer])
        if layer % N_MIXERS == 0:
            h = neighbourhood_attention(h, na_w_qkv[j], na_w_o[j], na_rpb[j])
        else:
            h = dilated_attention(h, dil_w_qkv[j], dil_w_o[j])
        x = x + rms_norm(h, norm_mix_post[layer])
        h = rms_norm(x, norm_ffn_pre[layer])
        x = x + rms_norm(swiglu(h, ffn_w_gate[layer], ffn_w_up[layer], ffn_w_down[layer]), norm_ffn_post[layer])
    return x
```

```python
import numpy as np
from contextlib import ExitStack
import concourse.bass as bass
import concourse.mybir as mybir
from concourse.bass_utils import run_bass_kernel_spmd

F32 = mybir.dt.float32
BF16 = mybir.dt.bfloat16
AF = mybir.ActivationFunctionType
ALU = mybir.AluOpType

P = 128
D = 1024
NCH = 8
NT = 2048
TB = 512
NTB = 4
DFF = 2816
NFC = 22
NH = 16
GRID_W = 64
NROWS = 32
DIL = (1, 4, 16)
NEG = -30000.0
EPS = 1e-6
NSLOT = 3
SLOT_ELEMS = 3072


class Sched:
    def __init__(self, nc, es):
        self.nc = nc
        self.es = es
        self.h = {"pe": nc.tensor, "act": nc.scalar, "dve": nc.vector, "pool": nc.gpsimd, "sp": nc.sync}
        self.sem = {}
        self.count = {}
        self.pending = {}
        for e in self.h:
            self.sem[e] = es.enter_context(nc.semaphore("sem_" + e))
            self.count[e] = 0
            self.pending[e] = False
        self.waited = {e: {} for e in self.h}
        self.lastw = {}
        self.readers = {}
        self.last_barrier = {}

    def dma_sem(self, name):
        if name not in self.sem:
            self.sem[name] = self.es.enter_context(self.nc.semaphore("dsem_" + name))
            self.count[name] = 0
        return name

    def _deps(self, eng, reads, writes):
        deps = {}

        def add(src, tok):
            if deps.get(src, 0) < tok:
                deps[src] = tok

        for k in reads:
            for src, tok in self.lastw.get(k, {}).items():
                add(src, tok)
            if isinstance(k, tuple) and k and k[0] == "ps":
                for src, tok in self.readers.get(k, {}).items():
                    if src != eng:
                        add(src, tok)
        for k in writes:
            for src, tok in self.lastw.get(k, {}).items():
                add(src, tok)
            for src, tok in self.readers.get(k, {}).items():
                add(src, tok)
        return deps

    def _wait(self, eng, deps):
        for src, tok in deps.items():
            if src == eng and eng == "pe":
                continue
            if self.waited[eng].get(src, 0) < tok:
                if src == eng:
                    assert tok <= self.count[eng], (eng, tok, self.count[eng])
                self.h[eng].wait_ge(self.sem[src], tok)
                self.waited[eng][src] = tok

    def _record(self, src, tok, reads, writes):
        for k in reads:
            self.readers.setdefault(k, {})[src] = tok
        for k in writes:
            self.lastw[k] = {src: tok}
            self.readers[k] = {}

    def op(self, eng, fn, reads=(), writes=(), sig=True):
        self._wait(eng, self._deps(eng, reads, writes))
        ins = fn()
        if sig:
            self.count[eng] += 1
            ins.then_inc(self.sem[eng], 1)
            tok = self.count[eng]
            self.pending[eng] = False
        else:
            tok = self.count[eng] + 1
            self.pending[eng] = True
        self._record(eng, tok, reads, writes)
        return tok

    def dma(self, queue, semname, fn, reads=(), writes=(), after_barrier=False):
        self.dma_sem(semname)
        deps = self._deps(semname, reads, writes)
        if after_barrier:
            for e, c in self.last_barrier.items():
                if deps.get(e, 0) < c:
                    deps[e] = c
        self._wait(queue, deps)
        ins = fn()
        self.count[semname] += 16
        ins.then_inc(self.sem[semname], 16)
        self._record(semname, self.count[semname], reads, writes)

    def barrier(self, engines=("pe", "act", "dve")):
        for e in engines:
            assert not self.pending[e], e
        for e in engines:
            for e2 in engines:
                if e2 != e and self.waited[e].get(e2, 0) < self.count[e2]:
                    self.h[e].wait_ge(self.sem[e2], self.count[e2])
                    self.waited[e][e2] = self.count[e2]
        self.last_barrier = {e: self.count[e] for e in engines if self.count[e] > 0}

    def wait_all(self, eng):
        for src, c in self.count.items():
            if c > 0 and src != eng and self.waited[eng].get(src, 0) < c:
                self.h[eng].wait_ge(self.sem[src], c)
                self.waited[eng][src] = c


def _alibi_slopes():
    return (2.0 ** (-8.0 * np.arange(1, NH + 1, dtype=np.float64) / NH)).astype(np.float32)


def _na_bias_table(rpb):
    col = np.arange(GRID_W)
    cs = np.clip(col - 8, 0, GRID_W - 16)
    cmask = (col[None, :] >= cs[:, None]) & (col[None, :] < cs[:, None] + 16)
    cidx = np.clip(col[None, :] - col[:, None] + 15, 0, 30)
    out = np.empty((8, 2, 64, 2, 2, 7, 64), np.float32)
    for s in range(2):
        for par in range(2):
            for m in range(7):
                row = 2 * m + par + s
                g = rpb[:, row][:, cidx]
                g = np.where(cmask[None], g, np.float32(NEG))
                g = g.reshape(8, 2, 64, 64)
                out[:, s, :, :, par, m, :] = g.transpose(0, 3, 1, 2)
    return np.ascontiguousarray(out.reshape(8, 128, 2 * 2 * 7 * 64))


def _dil_base_tile():
    p = np.arange(128)[:, None]
    q = np.arange(384)[None, :] - 128
    d = np.abs(q - p)
    return np.where(d <= 64, -d, NEG).astype(np.float32)


def _dil_scaled_ident():
    sl = _alibi_slopes()
    out = np.zeros((8, 128, 2, 3, 128), np.float32)
    eye = np.eye(128, dtype=np.float32)
    for j in range(8):
        for hh in range(2):
            for g in range(3):
                out[j, :, hh, g, :] = eye * np.float32(sl[2 * j + hh] * DIL[g])
    return np.ascontiguousarray(out.reshape(8, 128, 768))


def _host_layout(inputs, layers):
    f = lambda a: np.ascontiguousarray(a, dtype=np.float32)
    shared = {}
    gains = np.empty((128, 4, 2, NCH), np.float32)
    for kind, nm in enumerate(["norm_mix_pre", "norm_mix_post", "norm_ffn_pre", "norm_ffn_post"]):
        gains[:, kind] = inputs[nm].reshape(2, NCH, 128).transpose(2, 0, 1)
    shared["gains"] = f(gains.reshape(128, 64))
    shared["ident"] = f(np.eye(128, dtype=np.float32))
    if 0 in layers:
        w = inputs["na_w_qkv"][0].reshape(D, 3, 8, 128)
        shared["wqkv0"] = f(w.transpose(2, 0, 1, 3).reshape(8, D, 384))
        shared["wo0"] = f(inputs["na_w_o"][0])
        shared["nabias"] = _na_bias_table(np.asarray(inputs["na_rpb"][0], np.float32))
    if 1 in layers:
        w = inputs["dil_w_qkv"][0].reshape(D, 3, 3, 8, 128)
        shared["wqkv1"] = f(w.transpose(3, 1, 0, 2, 4).reshape(8, 3, D, 384))
        shared["wo1"] = f(inputs["dil_w_o"][0])
        shared["dbase"] = _dil_base_tile()
        shared["dsid"] = _dil_scaled_ident()
    for l in layers:
        g = inputs["ffn_w_gate"][l].reshape(D, NFC, 128)
        u = inputs["ffn_w_up"][l].reshape(D, NFC, 128)
        shared[f"wgu{l}"] = f(np.stack([g, u], axis=2).transpose(1, 0, 2, 3).reshape(NFC, D, 256))
        shared[f"wd{l}"] = f(inputs["ffn_w_down"][l].reshape(DFF, NCH, 128).transpose(1, 0, 2))
    return shared


def build(layers):
    nc = bass.Bass("TRN2", target_bir_lowering=False)
    dr = {}

    def din(name, shape):
        dr[name] = nc.dram_tensor(name, list(shape), F32, kind="ExternalInput").ap()

    din("xT", (D, NT))
    din("gains", (128, 64))
    din("ident", (128, 128))
    if 0 in layers:
        din("wqkv0", (8, D, 384))
        din("wo0", (D, D))
        din("nabias", (8, 128, 1792))
    if 1 in layers:
        din("wqkv1", (8, 3, D, 384))
        din("wo1", (D, D))
        din("dbase", (128, 384))
        din("dsid", (8, 128, 768))
    for l in layers:
        din(f"wgu{l}", (NFC, D, 256))
        din(f"wd{l}", (NCH, DFF, 128))
    outT = nc.dram_tensor("outT", [D, NT], F32, kind="ExternalOutput").ap()

    with ExitStack() as es:
        E = es.enter_context
        S = Sched(nc, es)
        sb = lambda name, shape, dt: E(nc.sbuf_tensor(name, list(shape), dt))

        xT = sb("xT_sb", (P, NCH, NT), F32)
        HH = {}
        gains = sb("gains_sb", (P, 64), F32)
        ones_bf = sb("ones_bf", (P, P), BF16)
        ident_bf = sb("ident_bf", (P, P), BF16)
        epsc = sb("epsc", (P, 1), F32)
        wslots = sb("wslots", (P, NSLOT, SLOT_ELEMS), BF16)
        sqb = sb("sqb", (P, 2, TB), BF16)
        rstd_t = sb("rstd_t", (P, 1, TB), F32)
        rstd = sb("rstd", (P, 2, TB), F32)
        utmp = sb("utmp", (P, 2, TB), F32)
        banks = [E(nc.psum_tensor(f"bank{i}", [P, TB], F32)) for i in range(8)]

        def tbs(tb):
            return slice(tb * TB, (tb + 1) * TB)

        def gain_ap(kind, layer, c):
            i = (kind * 2 + layer) * NCH + c
            return gains[:, i:i + 1]

        for c in range(NCH):
            S.dma("sp", "xload", lambda c=c: nc.sync.dma_start(out=xT[:, c, :], in_=dr["xT"][c * P:(c + 1) * P, :]))
        for c in range(NCH):
            for tb in range(NTB):
                S.lastw[("x", c, tb)] = {"xload": S.count["xload"]}
        S.dma("sp", "cload", lambda: nc.sync.dma_start(out=gains[:], in_=dr["gains"]), writes=[("gains",)])
        S.dma("pool", "cload2", lambda: nc.gpsimd.dma_start(out=ident_bf[:], in_=dr["ident"]), writes=[("ident",)])
        S.op("dve", lambda: nc.vector.memset(ones_bf[:], 1.0), writes=[("ones",)])
        S.op("dve", lambda: nc.vector.memset(epsc[:], EPS), writes=[("eps",)])

        slot_ctr = [0]

        def load_slab(src_ap, ncols, nk):
            s = slot_ctr[0] % NSLOT
            slot_ctr[0] += 1
            view = wslots[:, s, 0:nk * ncols].rearrange("p (k c) -> p k c", c=ncols)
            S.dma("pool", f"wslot{s}",
                  lambda: nc.gpsimd.dma_start(out=view, in_=src_ap.rearrange("(k p) c -> p k c", p=P)),
                  writes=[("wslot", s)])
            return view, ("wslot", s)

        sq_ctr = [0]
        ev_ctr = [0]

        def stat_accum(src_ap, src_key, stat_bank, stat_key, first, last):
            i = sq_ctr[0] % 2
            sq_ctr[0] += 1
            S.op("act", lambda: nc.scalar.activation(out=sqb[:, i, :], in_=src_ap, func=AF.Square),
                 reads=[src_key], writes=[("sq", i)])
            S.op("pe", lambda: nc.tensor.matmul(stat_bank[:, :], lhsT=ones_bf[:], rhs=sqb[:, i, :], start=first, stop=last),
                 reads=[("sq", i), ("ones",)], writes=[stat_key], sig=True)

        def finish_rstd(stat_bank, stat_key, ri):
            S.op("act", lambda: nc.scalar.activation(out=rstd_t[:, 0, :], in_=stat_bank[:, :], func=AF.Sqrt,
                                                     bias=epsc[:, 0:1], scale=1.0 / D),
                 reads=[stat_key, ("eps",)], writes=[("rstd_t", 0)])
            S.op("dve", lambda: nc.vector.reciprocal(out=rstd[:, ri, :], in_=rstd_t[:, 0, :]),
                 reads=[("rstd_t", 0)], writes=[("rstd", ri)])

        def pre_norm(layer, kind, tb_list, stat_bank_ids):
            for n, tb in enumerate(tb_list):
                bi = stat_bank_ids[n % len(stat_bank_ids)]
                bank, bkey = banks[bi], ("ps", bi)
                for c in range(NCH):
                    stat_accum(xT[:, c, tbs(tb)], ("x", c, tb), bank, bkey, c == 0, c == NCH - 1)
                ri = n % 2
                finish_rstd(bank, bkey, ri)
                for c in range(NCH):
                    S.op("dve", lambda c=c: nc.vector.scalar_tensor_tensor(
                        out=HH['h'](c, tb), in0=xT[:, c, tbs(tb)], scalar=gain_ap(kind, layer, c),
                        in1=rstd[:, ri, :], op0=ALU.mult, op1=ALU.mult),
                        reads=[("x", c, tb), ("rstd", ri), ("gains",)], writes=[("h", c, tb)])

        def post_norm_update(layer, kind, tb, ysb, ysb_keyf, ri):
            for c in range(NCH):
                ui = c % 2
                S.op("dve", lambda c=c, ui=ui: nc.vector.scalar_tensor_tensor(
                    out=utmp[:, ui, :], in0=ysb(c), scalar=gain_ap(kind, layer, c), in1=rstd[:, ri, :],
                    op0=ALU.mult, op1=ALU.mult),
                    reads=[ysb_keyf(c), ("rstd", ri), ("gains",)], writes=[("utmp", ui)])
                S.op("dve", lambda c=c, ui=ui: nc.vector.tensor_tensor(
                    out=xT[:, c, tbs(tb)], in0=xT[:, c, tbs(tb)], in1=utmp[:, ui, :], op=ALU.add),
                    reads=[("utmp", ui), ("x", c, tb)], writes=[("x", c, tb)])

        def evac_copy(out_ap, in_ap, reads, writes, scale=None):
            ev_ctr[0] += 1
            if ev_ctr[0] % 2 == 0:
                if scale is None:
                    S.op("act", lambda: nc.scalar.activation(out=out_ap, in_=in_ap, func=AF.Copy), reads=reads, writes=writes)
                else:
                    S.op("act", lambda: nc.scalar.activation(out=out_ap, in_=in_ap, func=AF.Copy, scale=scale), reads=reads, writes=writes)
            else:
                if scale is None:
                    S.op("dve", lambda: nc.vector.tensor_copy(out=out_ap, in_=in_ap), reads=reads, writes=writes)
                else:
                    S.op("dve", lambda: nc.vector.tensor_scalar(out=out_ap, in0=in_ap, scalar1=scale, scalar2=None, op0=ALU.mult),
                         reads=reads, writes=writes)

        def oproj_phase(layer, wo_ap, aoT):
            stopped = False
            with ExitStack() as ps_:
                ysb2 = ps_.enter_context(nc.sbuf_tensor(f"ysb_o{layer}", [P, 2, NCH, TB], F32))
                try:
                    slabs = []
                    for s3 in range(3):
                        ncols = 384 if s3 < 2 else 256
                        v, k = load_slab(wo_ap[:, s3 * 384: s3 * 384 + ncols], ncols, NCH)
                        slabs.append((v, k))
                    chk("oslab")
                    pb = [0]
                    for tb in range(NTB):
                        stat_bank, stat_key = banks[2 + tb % 2], ("ps", 2 + tb % 2)
                        for dc in range(NCH):
                            v, k = slabs[dc // 3]
                            co = (dc % 3) * P
                            bi = pb[0] % 2
                            pb[0] += 1
                            for kc in range(NCH):
                                S.op("pe", lambda kc=kc: nc.tensor.matmul(banks[bi][:, :], lhsT=v[:, kc, co:co + P],
                                                                             rhs=aoT[:, kc, tbs(tb)], start=kc == 0, stop=kc == NCH - 1),
                                     reads=[k, ("ao", kc, tb)], writes=[("ps", bi)], sig=kc == NCH - 1)
                            S.op("dve", lambda dc=dc, bi=bi: nc.vector.tensor_copy(out=ysb2[:, tb % 2, dc, :], in_=banks[bi][:, :]),
                                 reads=[("ps", bi)], writes=[("ysb", tb % 2, dc)])
                            stat_accum(banks[bi][:, :], ("ps", bi), stat_bank, stat_key, dc == 0, dc == NCH - 1)
                        chk("omm")
                        finish_rstd(stat_bank, stat_key, tb % 2)
                        chk("orstd")
                        post_norm_update(layer, 1, tb, lambda c, tb=tb: ysb2[:, tb % 2, c, :], lambda c, tb=tb: ("ysb", tb % 2, c), tb % 2)
                        chk("onorm")
                    S.barrier()
                except _Stop:
                    stopped = True
            if stopped:
                raise _Stop()

        def ffn_phase(layer):
            wgu, wd = dr[f"wgu{layer}"], dr[f"wd{layer}"]
            with ExitStack() as ps_:
                actT = ps_.enter_context(nc.sbuf_tensor(f"actT{layer}", [P, NFC, 2 * TB], BF16))
                ysb = ps_.enter_context(nc.sbuf_tensor(f"ysb_f{layer}", [P, NCH, 2 * TB], F32))
                sg = ps_.enter_context(nc.sbuf_tensor(f"sg{layer}", [P, 2, TB], F32))
                hTf = ps_.enter_context(nc.sbuf_tensor(f"hTf{layer}", [P, NCH, 2 * TB], BF16))
                HH['h'] = lambda c, tb: hTf[:, c, (tb % 2) * TB:(tb % 2 + 1) * TB]
                pre_norm(layer, 2, [0, 1], [6, 7])
                for half in range(2):
                    tb_list = [2 * half, 2 * half + 1]
                    n = 0
                    for fc in range(NFC):
                        v, k = load_slab(wgu[fc], 256, NCH)
                        for t2, tb in enumerate(tb_list):
                            bg, bu = n % 2, 2 + n % 2
                            n += 1
                            for kc in range(NCH):
                                S.op("pe", lambda kc=kc: nc.tensor.matmul(banks[bg][:, :], lhsT=v[:, kc, 0:P], rhs=HH['h'](kc, tb),
                                                                             start=kc == 0, stop=kc == NCH - 1),
                                     reads=[k, ("h", kc, tb)], writes=[("ps", bg)], sig=kc == NCH - 1)
                            for kc in range(NCH):
                                S.op("pe", lambda kc=kc: nc.tensor.matmul(banks[bu][:, :], lhsT=v[:, kc, P:2 * P], rhs=HH['h'](kc, tb),
                                                                             start=kc == 0, stop=kc == NCH - 1),
                                     reads=[k, ("h", kc, tb)], writes=[("ps", bu)], sig=kc == NCH - 1)
                            si = n % 2
                            S.op("act", lambda: nc.scalar.activation(out=sg[:, si, :], in_=banks[bg][:, :], func=AF.Silu),
                                 reads=[("ps", bg)], writes=[("sg", si)])
                            S.op("dve", lambda: nc.vector.tensor_tensor(out=actT[:, fc, t2 * TB:(t2 + 1) * TB], in0=sg[:, si, :],
                                                                        in1=banks[bu][:, :], op=ALU.mult),
                                 reads=[("sg", si), ("ps", bu)], writes=[("act", fc, t2)])
                    if half == 0:
                        pre_norm(layer, 2, [2, 3], [6, 7])
                    n = 0
                    for dc in range(NCH):
                        v, k = load_slab(wd[dc], P, NFC)
                        for t2, tb in enumerate(tb_list):
                            bi = 4 + n % 2
                            n += 1
                            for fc in range(NFC):
                                S.op("pe", lambda fc=fc: nc.tensor.matmul(banks[bi][:, :], lhsT=v[:, fc, :],
                                                                             rhs=actT[:, fc, t2 * TB:(t2 + 1) * TB],
                                                                             start=fc == 0, stop=fc == NFC - 1),
                                     reads=[k, ("act", fc, t2)], writes=[("ps", bi)], sig=fc == NFC - 1)
                            S.op("dve", lambda: nc.vector.tensor_copy(out=ysb[:, dc, t2 * TB:(t2 + 1) * TB], in_=banks[bi][:, :]),
                                 reads=[("ps", bi)], writes=[("ysbf", dc, t2)])
                            stat_accum(banks[bi][:, :], ("ps", bi), banks[6 + t2], ("ps", 6 + t2), dc == 0, dc == NCH - 1)
                    for t2, tb in enumerate(tb_list):
                        finish_rstd(banks[6 + t2], ("ps", 6 + t2), t2)
                        post_norm_update(layer, 3, tb, lambda c, t2=t2: ysb[:, c, t2 * TB:(t2 + 1) * TB],
                                         lambda c, t2=t2: ("ysbf", c, t2), t2)
                S.barrier()

        def project_fm(v, k, col0, dest_fn, dest_keys_fn, scale=None):
            for tb in range(NTB):
                bi = proj_ctr[0] % 2
                proj_ctr[0] += 1
                for kc in range(NCH):
                    S.op("pe", lambda kc=kc: nc.tensor.matmul(banks[bi][:, :], lhsT=v[:, kc, col0:col0 + P], rhs=HH['h'](kc, tb),
                                                                 start=kc == 0, stop=kc == NCH - 1),
                         reads=[k, ("h", kc, tb)], writes=[("ps", bi)], sig=kc == NCH - 1)
                evac_copy(dest_fn(tb), dest_src(banks[bi], tb), [("ps", bi)], dest_keys_fn(tb), scale=scale)

        proj_ctr = [0]
        dest_src_holder = [None]

        def dest_src(bank, tb):
            return dest_src_holder[0](bank, tb)

        def transpose_v(vT_fn, ntiles, vaug4, vkey_fn, vT_keys_fn):
            tbank = banks[7].bitcast(BF16)
            t = 0
            while t < ntiles:
                n = min(8, ntiles - t)
                for i in range(n):
                    S.op("pe", lambda i=i: nc.tensor.transpose(tbank[:, i * P:(i + 1) * P], vT_fn(t + i), ident_bf[:]),
                         reads=list(vT_keys_fn(t + i)) + [("ident",)], writes=[("ps", 7)], sig=i == n - 1)
                src = tbank[:, 0:n * P].rearrange("p (n h d) -> p n h d", h=2, d=64)
                S.op("dve", lambda t=t, n=n, src=src: nc.vector.tensor_copy(out=vaug4[:, t:t + n, 0:3:2, :], in_=src),
                     reads=[("ps", 7)], writes=[vkey_fn(t + i) for i in range(n)])
                t += n

        def normalize_pair(A_ap, B_ap, A_keys, B_keys, out_ap, out_keys, Rbuf, Rkey):
            S.op("act", lambda: nc.scalar.activation(out=Rbuf[0:64, :], in_=A_ap[64:128, :], func=AF.Ln),
                 reads=A_keys, writes=[(Rkey, 0)])
            S.op("act", lambda: nc.scalar.activation(out=Rbuf[64:128, :], in_=B_ap[0:64, :], func=AF.Ln),
                 reads=B_keys, writes=[(Rkey, 1)])
            S.op("act", lambda: nc.scalar.activation(out=Rbuf[:, :], in_=Rbuf[:, :], func=AF.Exp, scale=-1.0),
                 reads=[(Rkey, 0), (Rkey, 1)], writes=[(Rkey, 0), (Rkey, 1)])
            S.op("dve", lambda: nc.vector.tensor_tensor(out=out_ap[0:64, :], in0=A_ap[0:64, :], in1=Rbuf[0:64, :], op=ALU.mult),
                 reads=list(A_keys) + [(Rkey, 0)], writes=[out_keys[0]])
            S.op("dve", lambda: nc.vector.tensor_tensor(out=out_ap[64:128, :], in0=B_ap[64:128, :], in1=Rbuf[64:128, :], op=ALU.mult),
                 reads=list(B_keys) + [(Rkey, 1)], writes=[out_keys[1]])

        def na_phase(layer):
            with ExitStack() as ps_:
                A = ps_.enter_context
                aoT = A(nc.sbuf_tensor("aoT0", [P, NCH, NT], BF16))
                hT = A(nc.sbuf_tensor("hT0", [P, NCH, NT], BF16))
                HH['h'] = lambda c, tb: hT[:, c, tbs(tb)]
                inner = ExitStack()
                ps_.callback(inner.close)
                A = inner.enter_context
                stopped = False
                try:
                    qT = A(nc.sbuf_tensor("qT0", [P, NT], BF16))
                    kT = A(nc.sbuf_tensor("kT0", [P, NT], BF16))
                    vT = A(nc.sbuf_tensor("vT0", [P, NT], BF16))
                    vaug = A(nc.sbuf_tensor("vaug0", [P, 31, 3, 64], BF16))
                    biasb = A(nc.sbuf_tensor("nabias_sb", [P, 2, 1792], BF16))
                    PT = A(nc.sbuf_tensor("PT0", [P, 3, TB], BF16))
                    Rb = A(nc.sbuf_tensor("Rb0", [P, TB], F32))
                    S.op("dve", lambda: nc.vector.memset(vaug[:], 1.0), writes=[("vaug", t) for t in range(31)])
                    pre_norm(layer, 0, list(range(NTB)), [0, 1])
                    chk("prenorm")
                    dest_src_holder[0] = lambda bank, tb: bank[:, :]
                    unit_ctr = 0
                    for j in range(8):
                        bb = j % 2
                        S.dma("pool", f"nabias{bb}", lambda: nc.gpsimd.dma_start(out=biasb[:, bb, :], in_=dr["nabias"][j]),
                              writes=[("nabias", bb)], after_barrier=True)
                        v, k = load_slab(dr["wqkv0"][j], 384, NCH)
                        project_fm(v, k, 0, lambda tb: qT[:, tbs(tb)], lambda tb: [("q", tb)], scale=0.125)
                        project_fm(v, k, P, lambda tb: kT[:, tbs(tb)], lambda tb: [("k", tb)])
                        project_fm(v, k, 2 * P, lambda tb: vT[:, tbs(tb)], lambda tb: [("vT", tb)])
                        chk("proj")
                        transpose_v(lambda r0: vT[:, 64 * r0:64 * r0 + P], 31, vaug, lambda t: ("vaug", t),
                                    lambda r0: {("vT", (64 * r0) // TB), ("vT", (64 * r0 + P - 1) // TB)})
                        chk("transp")
                        bias5 = biasb[:, bb, :].rearrange("p (h a m c) -> p h a m c", h=2, a=2, m=7)
                        for tb in range(NTB):
                            units = [(il, hh) for il in range(8) for hh in range(2)]
                            Obank = {0: 5, 1: 6} if tb % 2 == 0 else {0: 4, 1: 7}
                            pend = None
                            groups = [units[u:u + 2] for u in range(0, 16, 2)]

                            def emit_S(gi, grp):
                                sbi = 2 + (gi % 2)
                                for ui, (il, hh) in enumerate(grp):
                                    i = 8 * tb + il
                                    rs = min(max(i - 4, 0), NROWS - 8)
                                    d0p = rs - i + 7
                                    par, m0 = d0p % 2, d0p // 2
                                    off = ui * 256
                                    rhs_b = bias5[:, hh, par, m0:m0 + 4, :].rearrange("p m c -> p (m c)")
                                    S.op("pe", lambda: nc.tensor.matmul(banks[sbi][:, off:off + 256], lhsT=ident_bf[:], rhs=rhs_b,
                                                                           start=True, stop=False),
                                         reads=[("ident",), ("nabias", bb)], writes=[("ps", sbi)], sig=False)
                                    for jj in range(4):
                                        k0 = 64 * (rs + 2 * jj)
                                        S.op("pe", lambda jj=jj, k0=k0: nc.tensor.matmul(
                                            banks[sbi][:, off + jj * 64: off + (jj + 1) * 64],
                                            lhsT=kT[hh * 64:(hh + 1) * 64, k0:k0 + P],
                                            rhs=qT[hh * 64:(hh + 1) * 64, 64 * i:64 * i + 64], start=False, stop=jj == 3),
                                            reads=[("k", k0 // TB), ("k", (k0 + P - 1) // TB), ("q", tb)], writes=[("ps", sbi)],
                                            sig=(jj == 3 and ui == len(grp) - 1))
                                pti = gi % 3
                                S.op("act", lambda: nc.scalar.activation(out=PT[:, pti, :], in_=banks[sbi][:, :], func=AF.Exp),
                                     reads=[("ps", sbi)], writes=[("PT", pti)])

                            def emit_PV(gi, grp):
                                pti = gi % 3
                                for ui, (il, hh) in enumerate(grp):
                                    i = 8 * tb + il
                                    rs = min(max(i - 4, 0), NROWS - 8)
                                    ob = Obank[hh]
                                    for jj in range(4):
                                        r0 = rs + 2 * jj
                                        last_unit = (il == 7 and jj == 3)
                                        S.op("pe", lambda jj=jj, r0=r0: nc.tensor.matmul(
                                            banks[ob][:, il * 64:(il + 1) * 64],
                                            lhsT=vaug[:, r0, hh:hh + 2, :].rearrange("p a d -> p (a d)"),
                                            rhs=PT[:, pti, ui * 256 + jj * 64: ui * 256 + (jj + 1) * 64],
                                            start=jj == 0, stop=jj == 3),
                                            reads=[("vaug", r0), ("PT", pti)], writes=[("ps", ob)], sig=(jj == 3))

                            emit_S(0, groups[0])
                            chk("S0")
                            for gi in range(len(groups)):
                                if gi + 1 < len(groups):
                                    emit_S(gi + 1, groups[gi + 1])
                                emit_PV(gi, groups[gi])
                            chk("PV")
                            normalize_pair(banks[Obank[0]], banks[Obank[1]], [("ps", Obank[0])], [("ps", Obank[1])], aoT[:, j, tbs(tb)],
                                           [("ao", j, tb), ("ao", j, tb)], Rb, "Rb")
                            chk("norm1")
                        chk("pair1")
                    chk("attn")
                    S.barrier()
                except _Stop:
                    stopped = True
                inner.close()
                if not stopped:
                    try:
                        oproj_phase(layer, dr["wo0"], aoT)
                    except _Stop:
                        stopped = True
            if stopped:
                raise _Stop()

        def dil_phase(layer):
            with ExitStack() as ps_:
                A = ps_.enter_context
                aoT = A(nc.sbuf_tensor("aoT1", [P, NCH, NT], BF16))
                hT = A(nc.sbuf_tensor("hT1", [P, NCH, NT], BF16))
                HH['h'] = lambda c, tb: hT[:, c, tbs(tb)]
                inner = ExitStack()
                ps_.callback(inner.close)
                A = inner.enter_context
                stopped = False
                try:
                    qT = A(nc.sbuf_tensor("qT1", [P, NT], BF16))
                    kT = A(nc.sbuf_tensor("kT1", [P, NT], BF16))
                    vT = A(nc.sbuf_tensor("vT1", [P, NT], BF16))
                    vaug = A(nc.sbuf_tensor("vaug1", [P, 16, 3, 64], BF16))
                    sid = A(nc.sbuf_tensor("sid_sb", [P, 2, 768], BF16))
                    dbase = A(nc.sbuf_tensor("dbase_sb", [P, 384], BF16))
                    acc = A(nc.sbuf_tensor("acc1", [P, 2, NT], F32))
                    PT = A(nc.sbuf_tensor("PT1", [P, 4, 384], BF16))
                    Rb = A(nc.sbuf_tensor("Rb1", [P, TB], F32))
                    S.op("dve", lambda: nc.vector.memset(vaug[:], 1.0), writes=[("vaug", t) for t in range(16)])
                    S.dma("pool", "dbase", lambda: nc.gpsimd.dma_start(out=dbase[:], in_=dr["dbase"]), writes=[("dbase",)], after_barrier=True)
                    pre_norm(layer, 0, list(range(NTB)), [0, 1])
                    for j in range(8):
                        bb = j % 2
                        S.dma("pool", f"sid{bb}", lambda: nc.gpsimd.dma_start(out=sid[:, bb, :], in_=dr["dsid"][j]), writes=[("sid", bb)], after_barrier=True)
                        sid4 = sid[:, bb, :].rearrange("p (h g c) -> p h g c", h=2, g=3)
                        for g, dil in enumerate(DIL):
                            L = NT // dil
                            nkt = L // P
                            mm = TB // dil
                            v, k = load_slab(dr["wqkv1"][j, g], 384, NCH)
                            dest_src_holder[0] = lambda bank, tb: bank[:, :].rearrange("p (m r) -> p r m", r=dil)

                            def dst(buf):
                                b3 = buf[:, :].rearrange("p (r m) -> p r m", r=dil)
                                return lambda tb: b3[:, :, tb * mm:(tb + 1) * mm]
                            project_fm(v, k, 0, dst(qT), lambda tb: [("q",)], scale=0.125)
                            project_fm(v, k, P, dst(kT), lambda tb: [("k",)])
                            project_fm(v, k, 2 * P, dst(vT), lambda tb: [("vT",)])
                            transpose_v(lambda t: vT[:, t * P:(t + 1) * P], 16, vaug, lambda t: ("vaug", t), lambda t: {("vT",)})
                            for hh in range(2):
                                hs = slice(hh * 64, (hh + 1) * 64)
                                s_ctr = [0]
                                o_ctr = [0]
                                SB = (2, 3, 6)

                                def emit_S(r, kt):
                                    base = r * L
                                    b_lo, b_hi = max(kt - 1, 0), min(kt + 1, nkt - 1)
                                    n = (b_hi - b_lo + 1) * P
                                    sbi = SB[s_ctr[0] % 3]
                                    pti = s_ctr[0] % 4
                                    s_ctr[0] += 1
                                    c0 = (b_lo - kt + 1) * P
                                    S.op("pe", lambda: nc.tensor.matmul(banks[sbi][:, 0:n], lhsT=sid4[:, hh, g, :], rhs=dbase[:, c0:c0 + n],
                                                                           start=True, stop=False),
                                         reads=[("sid", bb), ("dbase",)], writes=[("ps", sbi)], sig=False)
                                    S.op("pe", lambda: nc.tensor.matmul(banks[sbi][:, 0:n], lhsT=kT[hs, base + kt * P: base + (kt + 1) * P],
                                                                           rhs=qT[hs, base + b_lo * P: base + b_lo * P + n], start=False, stop=True),
                                         reads=[("k",), ("q",)], writes=[("ps", sbi)], sig=True)
                                    S.op("act", lambda: nc.scalar.activation(out=PT[:, pti, 0:n], in_=banks[sbi][:, 0:n], func=AF.Exp),
                                         reads=[("ps", sbi)], writes=[("PT", pti)])
                                    return (pti, b_lo)

                                def flush_O(ob, n4):
                                    if dil == 1:
                                        dst_ap = acc[:, hh, n4 * TB:(n4 + 1) * TB]
                                        src_ap = banks[ob][:, :]
                                    elif dil == 4:
                                        dst_ap = acc[:, hh, :].rearrange("p (m r) -> p r m", r=4)[:, n4, :]
                                        src_ap = banks[ob][:, :]
                                    else:
                                        dst_ap = acc[:, hh, :].rearrange("p (q r) -> p r q", r=16)[:, 4 * n4:4 * n4 + 4, :]
                                        src_ap = banks[ob][:, :].rearrange("p (r q) -> p r q", r=4)
                                    if g == 0:
                                        S.op("dve", lambda: nc.vector.tensor_copy(out=dst_ap, in_=src_ap),
                                             reads=[("ps", ob)], writes=[("acc", hh)])
                                    else:
                                        S.op("dve", lambda: nc.vector.tensor_tensor(out=dst_ap, in0=dst_ap, in1=src_ap, op=ALU.add),
                                             reads=[("ps", ob), ("acc", hh)], writes=[("acc", hh)])

                                def emit_PV(r, b, pts):
                                    gb = o_ctr[0]
                                    ob = 4 + (gb // 4) % 2
                                    col = (gb % 4) * P
                                    kts = [kt for kt in (b - 1, b, b + 1) if 0 <= kt < nkt]
                                    for n_, kt in enumerate(kts):
                                        pti, b_lo = pts[(r, kt)]
                                        S.op("pe", lambda kt=kt, pti=pti, b_lo=b_lo, n_=n_: nc.tensor.matmul(
                                            banks[ob][:, col:col + P],
                                            lhsT=vaug[:, r * nkt + kt, hh:hh + 2, :].rearrange("p a d -> p (a d)"),
                                            rhs=PT[:, pti, (b - b_lo) * P:(b - b_lo + 1) * P],
                                            start=n_ == 0, stop=n_ == len(kts) - 1),
                                            reads=[("vaug", r * nkt + kt), ("PT", pti)], writes=[("ps", ob)], sig=n_ == len(kts) - 1)
                                    o_ctr[0] += 1
                                    if o_ctr[0] % 4 == 0:
                                        flush_O(ob, o_ctr[0] // 4 - 1)

                                tiles = [(r, kt) for r in range(dil) for kt in range(nkt)]
                                LA = 2
                                pts = {}
                                for i in range(min(LA, len(tiles))):
                                    pts[tiles[i]] = emit_S(*tiles[i])
                                for i, (r, b) in enumerate(tiles):
                                    if i + LA < len(tiles):
                                        pts[tiles[i + LA]] = emit_S(*tiles[i + LA])
                                    emit_PV(r, b, pts)
                        for tb in range(NTB):
                            normalize_pair(acc[:, 0, tbs(tb)], acc[:, 1, tbs(tb)], [("acc", 0)], [("acc", 1)], aoT[:, j, tbs(tb)],
                                           [("ao", j, tb), ("ao", j, tb)], Rb, "Rb")
                    S.barrier()
                except _Stop:
                    stopped = True
                inner.close()
                if not stopped:
                    try:
                        oproj_phase(layer, dr["wo1"], aoT)
                    except _Stop:
                        stopped = True
            if stopped:
                raise _Stop()

        try:
            for layer in layers:
                if layer % 2 == 0:
                    na_phase(layer)
                else:
                    dil_phase(layer)
                chk("mixer")
                ffn_phase(layer)
        except _Stop:
            pass

        for c in range(NCH):
            S.dma("sp", "ostore", lambda c=c: nc.sync.dma_start(out=outT[c * P:(c + 1) * P, :], in_=xT[:, c, :]),
                  reads=[("x", c, tb) for tb in range(NTB)])
        S.wait_all("sp")
    return nc


_CACHE = {}
STOP = None


class _Stop(Exception):
    pass


def chk(name):
    if STOP == name:
        raise _Stop()


def _get_nc(layers):
    key = tuple(layers)
    if key not in _CACHE:
        _CACHE[key] = build(list(layers))
    return _CACHE[key]


FUSED = True


def _run(layers, xT_list, inputs):
    nc = _get_nc(layers)
    shared = _host_layout(inputs, layers)
    in_maps = [dict(shared, xT=xT_list[b]) for b in range(8)]
    res = run_bass_kernel_spmd(nc, in_maps, core_ids=list(range(8)))
    return [np.asarray(r["outT"]) for r in res.results]


def kernel(**inputs):
    inputs = {k: np.asarray(v) for k, v in inputs.items()}
    x = inputs["x"]
    xT = [np.ascontiguousarray(x[b].T) for b in range(8)]
    if FUSED:
        outT = _run((0, 1), xT, inputs)
    else:
        mid = _run((0,), xT, inputs)
        outT = _run((1,), mid, inputs)
    return np.ascontiguousarray(np.stack([o.T for o in outT], axis=0)).astype(np.float32)
```

```python
import numpy as np
from contextlib import ExitStack
import concourse.bass as bass
import concourse.mybir as mybir
from concourse.bass_utils import run_bass_kernel_spmd

F32 = mybir.dt.float32
BF16 = mybir.dt.bfloat16
AF = mybir.ActivationFunctionType
ALU = mybir.AluOpType

P = 128
D = 1024
NCH = 8
NT = 2048
TB = 512
NTB = 4
DFF = 2816
NFC = 22
NH = 16
GRID_W = 64
NROWS = 32
DIL = (1, 4, 16)
NEG = -30000.0
EPS = 1e-6
NSLOT = 3
SLOT_ELEMS = 3072


class Sched:
    def __init__(self, nc, es):
        self.nc = nc
        self.es = es
        self.h = {"pe": nc.tensor, "act": nc.scalar, "dve": nc.vector, "pool": nc.gpsimd, "sp": nc.sync}
        self.sem = {}
        self.count = {}
        self.pending = {}
        for e in self.h:
            self.sem[e] = es.enter_context(nc.semaphore("sem_" + e))
            self.count[e] = 0
            self.pending[e] = False
        self.waited = {e: {} for e in self.h}
        self.lastw = {}
        self.readers = {}
        self.last_barrier = {}

    def dma_sem(self, name):
        if name not in self.sem:
            self.sem[name] = self.es.enter_context(self.nc.semaphore("dsem_" + name))
            self.count[name] = 0
        return name

    def _deps(self, eng, reads, writes):
        deps = {}

        def add(src, tok):
            if deps.get(src, 0) < tok:
                deps[src] = tok

        for k in reads:
            for src, tok in self.lastw.get(k, {}).items():
                add(src, tok)
            if isinstance(k, tuple) and k and k[0] == "ps":
                for src, tok in self.readers.get(k, {}).items():
                    if src != eng:
                        add(src, tok)
        for k in writes:
            for src, tok in self.lastw.get(k, {}).items():
                add(src, tok)
            for src, tok in self.readers.get(k, {}).items():
                add(src, tok)
        return deps

    def _wait(self, eng, deps):
        for src, tok in deps.items():
            if src == eng and eng == "pe":
                continue
            if self.waited[eng].get(src, 0) < tok:
                if src == eng:
                    assert tok <= self.count[eng], (eng, tok, self.count[eng])
                self.h[eng].wait_ge(self.sem[src], tok)
                self.waited[eng][src] = tok

    def _record(self, src, tok, reads, writes):
        for k in reads:
            self.readers.setdefault(k, {})[src] = tok
        for k in writes:
            self.lastw[k] = {src: tok}
            self.readers[k] = {}

    def op(self, eng, fn, reads=(), writes=(), sig=True):
        self._wait(eng, self._deps(eng, reads, writes))
        ins = fn()
        if sig:
            self.count[eng] += 1
            ins.then_inc(self.sem[eng], 1)
            tok = self.count[eng]
            self.pending[eng] = False
        else:
            tok = self.count[eng] + 1
            self.pending[eng] = True
        self._record(eng, tok, reads, writes)
        return tok

    def dma(self, queue, semname, fn, reads=(), writes=(), after_barrier=False):
        self.dma_sem(semname)
        deps = self._deps(semname, reads, writes)
        if after_barrier:
            for e, c in self.last_barrier.items():
                if deps.get(e, 0) < c:
                    deps[e] = c
        self._wait(queue, deps)
        ins = fn()
        self.count[semname] += 16
        ins.then_inc(self.sem[semname], 16)
        self._record(semname, self.count[semname], reads, writes)

    def barrier(self, engines=("pe", "act", "dve")):
        for e in engines:
            assert not self.pending[e], e
        for e in engines:
            for e2 in engines:
                if e2 != e and self.waited[e].get(e2, 0) < self.count[e2]:
                    self.h[e].wait_ge(self.sem[e2], self.count[e2])
                    self.waited[e][e2] = self.count[e2]
        self.last_barrier = {e: self.count[e] for e in engines if self.count[e] > 0}

    def wait_all(self, eng):
        for src, c in self.count.items():
            if c > 0 and src != eng and self.waited[eng].get(src, 0) < c:
                self.h[eng].wait_ge(self.sem[src], c)
                self.waited[eng][src] = c


def _alibi_slopes():
    return (2.0 ** (-8.0 * np.arange(1, NH + 1, dtype=np.float64) / NH)).astype(np.float32)


def _na_bias_table(rpb):
    col = np.arange(GRID_W)
    cs = np.clip(col - 8, 0, GRID_W - 16)
    cmask = (col[None, :] >= cs[:, None]) & (col[None, :] < cs[:, None] + 16)
    cidx = np.clip(col[None, :] - col[:, None] + 15, 0, 30)
    out = np.empty((8, 2, 64, 2, 2, 7, 64), np.float32)
    for s in range(2):
        for par in range(2):
            for m in range(7):
                row = 2 * m + par + s
                g = rpb[:, row][:, cidx]
                g = np.where(cmask[None], g, np.float32(NEG))
                g = g.reshape(8, 2, 64, 64)
                out[:, s, :, :, par, m, :] = g.transpose(0, 3, 1, 2)
    return np.ascontiguousarray(out.reshape(8, 128, 2 * 2 * 7 * 64))


def _dil_base_tile():
    p = np.arange(128)[:, None]
    q = np.arange(384)[None, :] - 128
    d = np.abs(q - p)
    return np.where(d <= 64, -d, NEG).astype(np.float32)


def _dil_scaled_ident():
    sl = _alibi_slopes()
    out = np.zeros((8, 128, 2, 3, 128), np.float32)
    eye = np.eye(128, dtype=np.float32)
    for j in range(8):
        for hh in range(2):
            for g in range(3):
                out[j, :, hh, g, :] = eye * np.float32(sl[2 * j + hh] * DIL[g])
    return np.ascontiguousarray(out.reshape(8, 128, 768))


def _host_layout(inputs, layers):
    f = lambda a: np.ascontiguousarray(a, dtype=np.float32)
    shared = {}
    gains = np.empty((128, 4, 2, NCH), np.float32)
    for kind, nm in enumerate(["norm_mix_pre", "norm_mix_post", "norm_ffn_pre", "norm_ffn_post"]):
        gains[:, kind] = inputs[nm].reshape(2, NCH, 128).transpose(2, 0, 1)
    shared["gains"] = f(gains.reshape(128, 64))
    shared["ident"] = f(np.eye(128, dtype=np.float32))
    if 0 in layers:
        w = inputs["na_w_qkv"][0].reshape(D, 3, 8, 128)
        shared["wqkv0"] = f(w.transpose(2, 0, 1, 3).reshape(8, D, 384))
        shared["wo0"] = f(inputs["na_w_o"][0])
        shared["nabias"] = _na_bias_table(np.asarray(inputs["na_rpb"][0], np.float32))
    if 1 in layers:
        w = inputs["dil_w_qkv"][0].reshape(D, 3, 3, 8, 128)
        shared["wqkv1"] = f(w.transpose(3, 1, 0, 2, 4).reshape(8, 3, D, 384))
        shared["wo1"] = f(inputs["dil_w_o"][0])
        shared["dbase"] = _dil_base_tile()
        shared["dsid"] = _dil_scaled_ident()
    for l in layers:
        g = inputs["ffn_w_gate"][l].reshape(D, NFC, 128)
        u = inputs["ffn_w_up"][l].reshape(D, NFC, 128)
        shared[f"wgu{l}"] = f(np.stack([g, u], axis=2).transpose(1, 0, 2, 3).reshape(NFC, D, 256))
        shared[f"wd{l}"] = f(inputs["ffn_w_down"][l].reshape(DFF, NCH, 128).transpose(1, 0, 2))
    return shared


def build(layers):
    nc = bass.Bass("TRN2", target_bir_lowering=False)
    dr = {}

    def din(name, shape):
        dr[name] = nc.dram_tensor(name, list(shape), F32, kind="ExternalInput").ap()

    din("xT", (D, NT))
    din("gains", (128, 64))
    din("ident", (128, 128))
    if 0 in layers:
        din("wqkv0", (8, D, 384))
        din("wo0", (D, D))
        din("nabias", (8, 128, 1792))
    if 1 in layers:
        din("wqkv1", (8, 3, D, 384))
        din("wo1", (D, D))
        din("dbase", (128, 384))
        din("dsid", (8, 128, 768))
    for l in layers:
        din(f"wgu{l}", (NFC, D, 256))
        din(f"wd{l}", (NCH, DFF, 128))
    outT = nc.dram_tensor("outT", [D, NT], F32, kind="ExternalOutput").ap()

    with ExitStack() as es:
        E = es.enter_context
        S = Sched(nc, es)
        sb = lambda name, shape, dt: E(nc.sbuf_tensor(name, list(shape), dt))

        xT = sb("xT_sb", (P, NCH, NT), F32)
        HH = {}
        gains = sb("gains_sb", (P, 64), F32)
        ones_bf = sb("ones_bf", (P, P), BF16)
        ident_bf = sb("ident_bf", (P, P), BF16)
        epsc = sb("epsc", (P, 1), F32)
        wslots = sb("wslots", (P, NSLOT, SLOT_ELEMS), BF16)
        sqb = sb("sqb", (P, 2, TB), BF16)
        rstd_t = sb("rstd_t", (P, 1, TB), F32)
        rstd = sb("rstd", (P, 2, TB), F32)
        utmp = sb("utmp", (P, 2, TB), F32)
        banks = [E(nc.psum_tensor(f"bank{i}", [P, TB], F32)) for i in range(8)]

        def tbs(tb):
            return slice(tb * TB, (tb + 1) * TB)

        def gain_ap(kind, layer, c):
            i = (kind * 2 + layer) * NCH + c
            return gains[:, i:i + 1]

        for c in range(NCH):
            S.dma("sp", "xload", lambda c=c: nc.sync.dma_start(out=xT[:, c, :], in_=dr["xT"][c * P:(c + 1) * P, :]))
        for c in range(NCH):
            for tb in range(NTB):
                S.lastw[("x", c, tb)] = {"xload": S.count["xload"]}
        S.dma("sp", "cload", lambda: nc.sync.dma_start(out=gains[:], in_=dr["gains"]), writes=[("gains",)])
        S.dma("pool", "cload2", lambda: nc.gpsimd.dma_start(out=ident_bf[:], in_=dr["ident"]), writes=[("ident",)])
        S.op("dve", lambda: nc.vector.memset(ones_bf[:], 1.0), writes=[("ones",)])
        S.op("dve", lambda: nc.vector.memset(epsc[:], EPS), writes=[("eps",)])

        slot_ctr = [0]

        def load_slab(src_ap, ncols, nk):
            s = slot_ctr[0] % NSLOT
            slot_ctr[0] += 1
            view = wslots[:, s, 0:nk * ncols].rearrange("p (k c) -> p k c", c=ncols)
            S.dma("pool", f"wslot{s}",
                  lambda: nc.gpsimd.dma_start(out=view, in_=src_ap.rearrange("(k p) c -> p k c", p=P)),
                  writes=[("wslot", s)])
            return view, ("wslot", s)

        sq_ctr = [0]
        ev_ctr = [0]

        def stat_square(src_ap, src_key):
            i = sq_ctr[0] % 2
            sq_ctr[0] += 1
            S.op("act", lambda: nc.scalar.activation(out=sqb[:, i, :], in_=src_ap, func=AF.Square),
                 reads=[src_key], writes=[("sq", i)])
            return i

        def stat_mm(i, stat_bank, stat_key, first, last):
            S.op("pe", lambda: nc.tensor.matmul(stat_bank[:, :], lhsT=ones_bf[:], rhs=sqb[:, i, :], start=first, stop=last),
                 reads=[("sq", i), ("ones",)], writes=[stat_key], sig=True)

        def stat_accum(src_ap, src_key, stat_bank, stat_key, first, last):
            stat_mm(stat_square(src_ap, src_key), stat_bank, stat_key, first, last)

        def finish_rstd(stat_bank, stat_key, ri):
            S.op("act", lambda: nc.scalar.activation(out=rstd_t[:, 0, :], in_=stat_bank[:, :], func=AF.Sqrt,
                                                     bias=epsc[:, 0:1], scale=1.0 / D),
                 reads=[stat_key, ("eps",)], writes=[("rstd_t", 0)])
            S.op("dve", lambda: nc.vector.reciprocal(out=rstd[:, ri, :], in_=rstd_t[:, 0, :]),
                 reads=[("rstd_t", 0)], writes=[("rstd", ri)])

        def pre_norm(layer, kind, tb_list, stat_bank_ids):
            for n, tb in enumerate(tb_list):
                bi = stat_bank_ids[n % len(stat_bank_ids)]
                bank, bkey = banks[bi], ("ps", bi)
                for c in range(NCH):
                    stat_accum(xT[:, c, tbs(tb)], ("x", c, tb), bank, bkey, c == 0, c == NCH - 1)
                ri = n % 2
                finish_rstd(bank, bkey, ri)
                for c in range(NCH):
                    S.op("dve", lambda c=c: nc.vector.scalar_tensor_tensor(
                        out=HH['h'](c, tb), in0=xT[:, c, tbs(tb)], scalar=gain_ap(kind, layer, c),
                        in1=rstd[:, ri, :], op0=ALU.mult, op1=ALU.mult),
                        reads=[("x", c, tb), ("rstd", ri), ("gains",)], writes=[("h", c, tb)])

        def post_norm_update(layer, kind, tb, ysb, ysb_keyf, ri):
            for c in range(NCH):
                ui = c % 2
                S.op("dve", lambda c=c, ui=ui: nc.vector.scalar_tensor_tensor(
                    out=utmp[:, ui, :], in0=ysb(c), scalar=gain_ap(kind, layer, c), in1=rstd[:, ri, :],
                    op0=ALU.mult, op1=ALU.mult),
                    reads=[ysb_keyf(c), ("rstd", ri), ("gains",)], writes=[("utmp", ui)])
                S.op("dve", lambda c=c, ui=ui: nc.vector.tensor_tensor(
                    out=xT[:, c, tbs(tb)], in0=xT[:, c, tbs(tb)], in1=utmp[:, ui, :], op=ALU.add),
                    reads=[("utmp", ui), ("x", c, tb)], writes=[("x", c, tb)])

        def evac_copy(out_ap, in_ap, reads, writes, scale=None):
            ev_ctr[0] += 1
            if ev_ctr[0] % 2 == 0:
                if scale is None:
                    S.op("act", lambda: nc.scalar.activation(out=out_ap, in_=in_ap, func=AF.Copy), reads=reads, writes=writes)
                else:
                    S.op("act", lambda: nc.scalar.activation(out=out_ap, in_=in_ap, func=AF.Copy, scale=scale), reads=reads, writes=writes)
            else:
                if scale is None:
                    S.op("dve", lambda: nc.vector.tensor_copy(out=out_ap, in_=in_ap), reads=reads, writes=writes)
                else:
                    S.op("dve", lambda: nc.vector.tensor_scalar(out=out_ap, in0=in_ap, scalar1=scale, scalar2=None, op0=ALU.mult),
                         reads=reads, writes=writes)

        def oproj_phase(layer, wo_ap, aoT):
            stopped = False
            with ExitStack() as ps_:
                ysb2 = ps_.enter_context(nc.sbuf_tensor(f"ysb_o{layer}", [P, 2, NCH, TB], F32))
                try:
                    slabs = []
                    for s3 in range(3):
                        ncols = 384 if s3 < 2 else 256
                        v, k = load_slab(wo_ap[:, s3 * 384: s3 * 384 + ncols], ncols, NCH)
                        slabs.append((v, k))
                    chk("oslab")
                    pb = [0]
                    for tb in range(NTB):
                        stat_bank, stat_key = banks[2 + tb % 2], ("ps", 2 + tb % 2)
                        pend = None
                        for dc in range(NCH):
                            v, k = slabs[dc // 3]
                            co = (dc % 3) * P
                            bi = pb[0] % 2
                            pb[0] += 1
                            for kc in range(NCH):
                                S.op("pe", lambda kc=kc: nc.tensor.matmul(banks[bi][:, :], lhsT=v[:, kc, co:co + P],
                                                                             rhs=aoT[:, kc, tbs(tb)], start=kc == 0, stop=kc == NCH - 1),
                                     reads=[k, ("ao", kc, tb)], writes=[("ps", bi)], sig=kc == NCH - 1)
                            if pend is not None:
                                stat_mm(*pend)
                            S.op("dve", lambda dc=dc, bi=bi: nc.vector.tensor_copy(out=ysb2[:, tb % 2, dc, :], in_=banks[bi][:, :]),
                                 reads=[("ps", bi)], writes=[("ysb", tb % 2, dc)])
                            pend = (stat_square(banks[bi][:, :], ("ps", bi)), stat_bank, stat_key, dc == 0, dc == NCH - 1)
                        stat_mm(*pend)
                        chk("omm")
                        finish_rstd(stat_bank, stat_key, tb % 2)
                        chk("orstd")
                        post_norm_update(layer, 1, tb, lambda c, tb=tb: ysb2[:, tb % 2, c, :], lambda c, tb=tb: ("ysb", tb % 2, c), tb % 2)
                        chk("onorm")
                    S.barrier()
                except _Stop:
                    stopped = True
            if stopped:
                raise _Stop()

        def ffn_phase(layer):
            wgu, wd = dr[f"wgu{layer}"], dr[f"wd{layer}"]
            with ExitStack() as ps_:
                actT = ps_.enter_context(nc.sbuf_tensor(f"actT{layer}", [P, NFC, 2 * TB], BF16))
                ysb = ps_.enter_context(nc.sbuf_tensor(f"ysb_f{layer}", [P, NCH, 2 * TB], F32))
                sg = ps_.enter_context(nc.sbuf_tensor(f"sg{layer}", [P, 2, TB], F32))
                hTf = ps_.enter_context(nc.sbuf_tensor(f"hTf{layer}", [P, NCH, 2 * TB], BF16))
                HH['h'] = lambda c, tb: hTf[:, c, (tb % 2) * TB:(tb % 2 + 1) * TB]
                pre_norm(layer, 2, [0, 1], [6, 7])
                for half in range(2):
                    tb_list = [2 * half, 2 * half + 1]
                    n = 0
                    for fc in range(NFC):
                        v, k = load_slab(wgu[fc], 256, NCH)
                        for t2, tb in enumerate(tb_list):
                            bg, bu = n % 2, 2 + n % 2
                            n += 1
                            for kc in range(NCH):
                                S.op("pe", lambda kc=kc: nc.tensor.matmul(banks[bg][:, :], lhsT=v[:, kc, 0:P], rhs=HH['h'](kc, tb),
                                                                             start=kc == 0, stop=kc == NCH - 1),
                                     reads=[k, ("h", kc, tb)], writes=[("ps", bg)], sig=kc == NCH - 1)
                            for kc in range(NCH):
                                S.op("pe", lambda kc=kc: nc.tensor.matmul(banks[bu][:, :], lhsT=v[:, kc, P:2 * P], rhs=HH['h'](kc, tb),
                                                                             start=kc == 0, stop=kc == NCH - 1),
                                     reads=[k, ("h", kc, tb)], writes=[("ps", bu)], sig=kc == NCH - 1)
                            si = n % 2
                            S.op("act", lambda: nc.scalar.activation(out=sg[:, si, :], in_=banks[bg][:, :], func=AF.Silu),
                                 reads=[("ps", bg)], writes=[("sg", si)])
                            S.op("dve", lambda: nc.vector.tensor_tensor(out=actT[:, fc, t2 * TB:(t2 + 1) * TB], in0=sg[:, si, :],
                                                                        in1=banks[bu][:, :], op=ALU.mult),
                                 reads=[("sg", si), ("ps", bu)], writes=[("act", fc, t2)])
                    if half == 0:
                        pre_norm(layer, 2, [2, 3], [6, 7])
                    n = 0
                    pend = None
                    for dc in range(NCH):
                        v, k = load_slab(wd[dc], P, NFC)
                        for t2, tb in enumerate(tb_list):
                            bi = 4 + n % 2
                            n += 1
                            for fc in range(NFC):
                                S.op("pe", lambda fc=fc: nc.tensor.matmul(banks[bi][:, :], lhsT=v[:, fc, :],
                                                                             rhs=actT[:, fc, t2 * TB:(t2 + 1) * TB],
                                                                             start=fc == 0, stop=fc == NFC - 1),
                                     reads=[k, ("act", fc, t2)], writes=[("ps", bi)], sig=fc == NFC - 1)
                            if pend is not None:
                                stat_mm(*pend)
                            S.op("dve", lambda: nc.vector.tensor_copy(out=ysb[:, dc, t2 * TB:(t2 + 1) * TB], in_=banks[bi][:, :]),
                                 reads=[("ps", bi)], writes=[("ysbf", dc, t2)])
                            pend = (stat_square(banks[bi][:, :], ("ps", bi)), banks[6 + t2], ("ps", 6 + t2), dc == 0, dc == NCH - 1)
                    stat_mm(*pend)
                    for t2, tb in enumerate(tb_list):
                        finish_rstd(banks[6 + t2], ("ps", 6 + t2), t2)
                        post_norm_update(layer, 3, tb, lambda c, t2=t2: ysb[:, c, t2 * TB:(t2 + 1) * TB],
                                         lambda c, t2=t2: ("ysbf", c, t2), t2)
                S.barrier()

        def project_fm(v, k, col0, dest_fn, dest_keys_fn, scale=None):
            for tb in range(NTB):
                bi = proj_ctr[0] % 2
                proj_ctr[0] += 1
                for kc in range(NCH):
                    S.op("pe", lambda kc=kc: nc.tensor.matmul(banks[bi][:, :], lhsT=v[:, kc, col0:col0 + P], rhs=HH['h'](kc, tb),
                                                                 start=kc == 0, stop=kc == NCH - 1),
                         reads=[k, ("h", kc, tb)], writes=[("ps", bi)], sig=kc == NCH - 1)
                evac_copy(dest_fn(tb), dest_src(banks[bi], tb), [("ps", bi)], dest_keys_fn(tb), scale=scale)

        proj_ctr = [0]
        dest_src_holder = [None]

        def dest_src(bank, tb):
            return dest_src_holder[0](bank, tb)

        def transpose_v(vT_fn, ntiles, vaug4, vkey_fn, vT_keys_fn):
            tbank = banks[7].bitcast(BF16)
            t = 0
            while t < ntiles:
                n = min(8, ntiles - t)
                for i in range(n):
                    S.op("pe", lambda i=i: nc.tensor.transpose(tbank[:, i * P:(i + 1) * P], vT_fn(t + i), ident_bf[:]),
                         reads=list(vT_keys_fn(t + i)) + [("ident",)], writes=[("ps", 7)], sig=i == n - 1)
                src = tbank[:, 0:n * P].rearrange("p (n h d) -> p n h d", h=2, d=64)
                S.op("dve", lambda t=t, n=n, src=src: nc.vector.tensor_copy(out=vaug4[:, t:t + n, 0:3:2, :], in_=src),
                     reads=[("ps", 7)], writes=[vkey_fn(t + i) for i in range(n)])
                t += n

        def normalize_pair(A_ap, B_ap, A_keys, B_keys, out_ap, out_keys, Rbuf, Rkey):
            S.op("act", lambda: nc.scalar.activation(out=Rbuf[0:64, :], in_=A_ap[64:128, :], func=AF.Ln),
                 reads=A_keys, writes=[(Rkey, 0)])
            S.op("act", lambda: nc.scalar.activation(out=Rbuf[64:128, :], in_=B_ap[0:64, :], func=AF.Ln),
                 reads=B_keys, writes=[(Rkey, 1)])
            S.op("act", lambda: nc.scalar.activation(out=Rbuf[:, :], in_=Rbuf[:, :], func=AF.Exp, scale=-1.0),
                 reads=[(Rkey, 0), (Rkey, 1)], writes=[(Rkey, 0), (Rkey, 1)])
            S.op("dve", lambda: nc.vector.tensor_tensor(out=out_ap[0:64, :], in0=A_ap[0:64, :], in1=Rbuf[0:64, :], op=ALU.mult),
                 reads=list(A_keys) + [(Rkey, 0)], writes=[out_keys[0]])
            S.op("dve", lambda: nc.vector.tensor_tensor(out=out_ap[64:128, :], in0=B_ap[64:128, :], in1=Rbuf[64:128, :], op=ALU.mult),
                 reads=list(B_keys) + [(Rkey, 1)], writes=[out_keys[1]])

        def na_phase(layer):
            with ExitStack() as ps_:
                A = ps_.enter_context
                aoT = A(nc.sbuf_tensor("aoT0", [P, NCH, NT], BF16))
                hT = A(nc.sbuf_tensor("hT0", [P, NCH, NT], BF16))
                HH['h'] = lambda c, tb: hT[:, c, tbs(tb)]
                inner = ExitStack()
                ps_.callback(inner.close)
                A = inner.enter_context
                stopped = False
                try:
                    qT = A(nc.sbuf_tensor("qT0", [P, NT], BF16))
                    kT = A(nc.sbuf_tensor("kT0", [P, NT], BF16))
                    vT = A(nc.sbuf_tensor("vT0", [P, NT], BF16))
                    vaug = A(nc.sbuf_tensor("vaug0", [P, 31, 3, 64], BF16))
                    biasb = A(nc.sbuf_tensor("nabias_sb", [P, 2, 1792], BF16))
                    PT = A(nc.sbuf_tensor("PT0", [P, 3, TB], BF16))
                    Rb = A(nc.sbuf_tensor("Rb0", [P, TB], F32))
                    S.op("dve", lambda: nc.vector.memset(vaug[:], 1.0), writes=[("vaug", t) for t in range(31)])
                    pre_norm(layer, 0, list(range(NTB)), [0, 1])
                    chk("prenorm")
                    dest_src_holder[0] = lambda bank, tb: bank[:, :]
                    unit_ctr = 0
                    for j in range(8):
                        bb = j % 2
                        S.dma("pool", f"nabias{bb}", lambda: nc.gpsimd.dma_start(out=biasb[:, bb, :], in_=dr["nabias"][j]),
                              writes=[("nabias", bb)], after_barrier=True)
                        v, k = load_slab(dr["wqkv0"][j], 384, NCH)
                        project_fm(v, k, 0, lambda tb: qT[:, tbs(tb)], lambda tb: [("q", tb)], scale=0.125)
                        project_fm(v, k, P, lambda tb: kT[:, tbs(tb)], lambda tb: [("k", tb)])
                        project_fm(v, k, 2 * P, lambda tb: vT[:, tbs(tb)], lambda tb: [("vT", tb)])
                        chk("proj")
                        transpose_v(lambda r0: vT[:, 64 * r0:64 * r0 + P], 31, vaug, lambda t: ("vaug", t),
                                    lambda r0: {("vT", (64 * r0) // TB), ("vT", (64 * r0 + P - 1) // TB)})
                        chk("transp")
                        bias5 = biasb[:, bb, :].rearrange("p (h a m c) -> p h a m c", h=2, a=2, m=7)
                        for tb in range(NTB):
                            units = [(il, hh) for il in range(8) for hh in range(2)]
                            Obank = {0: 5, 1: 6} if tb % 2 == 0 else {0: 4, 1: 7}
                            pend = None
                            groups = [units[u:u + 2] for u in range(0, 16, 2)]

                            def emit_S(gi, grp):
                                sbi = 2 + (gi % 2)
                                for ui, (il, hh) in enumerate(grp):
                                    i = 8 * tb + il
                                    rs = min(max(i - 4, 0), NROWS - 8)
                                    d0p = rs - i + 7
                                    par, m0 = d0p % 2, d0p // 2
                                    off = ui * 256
                                    rhs_b = bias5[:, hh, par, m0:m0 + 4, :].rearrange("p m c -> p (m c)")
                                    S.op("pe", lambda: nc.tensor.matmul(banks[sbi][:, off:off + 256], lhsT=ident_bf[:], rhs=rhs_b,
                                                                           start=True, stop=False),
                                         reads=[("ident",), ("nabias", bb)], writes=[("ps", sbi)], sig=False)
                                    for jj in range(4):
                                        k0 = 64 * (rs + 2 * jj)
                                        S.op("pe", lambda jj=jj, k0=k0: nc.tensor.matmul(
                                            banks[sbi][:, off + jj * 64: off + (jj + 1) * 64],
                                            lhsT=kT[hh * 64:(hh + 1) * 64, k0:k0 + P],
                                            rhs=qT[hh * 64:(hh + 1) * 64, 64 * i:64 * i + 64], start=False, stop=jj == 3),
                                            reads=[("k", k0 // TB), ("k", (k0 + P - 1) // TB), ("q", tb)], writes=[("ps", sbi)],
                                            sig=(jj == 3 and ui == len(grp) - 1))
                                pti = gi % 3
                                S.op("act", lambda: nc.scalar.activation(out=PT[:, pti, :], in_=banks[sbi][:, :], func=AF.Exp),
                                     reads=[("ps", sbi)], writes=[("PT", pti)])

                            def emit_PV(gi, grp):
                                pti = gi % 3
                                for ui, (il, hh) in enumerate(grp):
                                    i = 8 * tb + il
                                    rs = min(max(i - 4, 0), NROWS - 8)
                                    ob = Obank[hh]
                                    for jj in range(4):
                                        r0 = rs + 2 * jj
                                        last_unit = (il == 7 and jj == 3)
                                        S.op("pe", lambda jj=jj, r0=r0: nc.tensor.matmul(
                                            banks[ob][:, il * 64:(il + 1) * 64],
                                            lhsT=vaug[:, r0, hh:hh + 2, :].rearrange("p a d -> p (a d)"),
                                            rhs=PT[:, pti, ui * 256 + jj * 64: ui * 256 + (jj + 1) * 64],
                                            start=jj == 0, stop=jj == 3),
                                            reads=[("vaug", r0), ("PT", pti)], writes=[("ps", ob)], sig=(jj == 3))

                            emit_S(0, groups[0])
                            chk("S0")
                            for gi in range(len(groups)):
                                if gi + 1 < len(groups):
                                    emit_S(gi + 1, groups[gi + 1])
                                emit_PV(gi, groups[gi])
                            chk("PV")
                            normalize_pair(banks[Obank[0]], banks[Obank[1]], [("ps", Obank[0])], [("ps", Obank[1])], aoT[:, j, tbs(tb)],
                                           [("ao", j, tb), ("ao", j, tb)], Rb, "Rb")
                            chk("norm1")
                        chk("pair1")
                    chk("attn")
                    S.barrier()
                except _Stop:
                    stopped = True
                inner.close()
                if not stopped:
                    try:
                        oproj_phase(layer, dr["wo0"], aoT)
                    except _Stop:
                        stopped = True
            if stopped:
                raise _Stop()

        def dil_phase(layer):
            with ExitStack() as ps_:
                A = ps_.enter_context
                aoT = A(nc.sbuf_tensor("aoT1", [P, NCH, NT], BF16))
                hT = A(nc.sbuf_tensor("hT1", [P, NCH, NT], BF16))
                HH['h'] = lambda c, tb: hT[:, c, tbs(tb)]
                inner = ExitStack()
                ps_.callback(inner.close)
                A = inner.enter_context
                stopped = False
                try:
                    qT = A(nc.sbuf_tensor("qT1", [P, NT], BF16))
                    kT = A(nc.sbuf_tensor("kT1", [P, NT], BF16))
                    vT = A(nc.sbuf_tensor("vT1", [P, NT], BF16))
                    vaug = A(nc.sbuf_tensor("vaug1", [P, 16, 3, 64], BF16))
                    sid = A(nc.sbuf_tensor("sid_sb", [P, 2, 768], BF16))
                    dbase = A(nc.sbuf_tensor("dbase_sb", [P, 384], BF16))
                    acc = A(nc.sbuf_tensor("acc1", [P, 2, NT], F32))
                    PT = A(nc.sbuf_tensor("PT1", [P, 4, 384], BF16))
                    Rb = A(nc.sbuf_tensor("Rb1", [P, TB], F32))
                    S.op("dve", lambda: nc.vector.memset(vaug[:], 1.0), writes=[("vaug", t) for t in range(16)])
                    S.dma("pool", "dbase", lambda: nc.gpsimd.dma_start(out=dbase[:], in_=dr["dbase"]), writes=[("dbase",)], after_barrier=True)
                    pre_norm(layer, 0, list(range(NTB)), [0, 1])
                    for j in range(8):
                        bb = j % 2
                        S.dma("pool", f"sid{bb}", lambda: nc.gpsimd.dma_start(out=sid[:, bb, :], in_=dr["dsid"][j]), writes=[("sid", bb)], after_barrier=True)
                        sid4 = sid[:, bb, :].rearrange("p (h g c) -> p h g c", h=2, g=3)
                        for g, dil in enumerate(DIL):
                            L = NT // dil
                            nkt = L // P
                            mm = TB // dil
                            v, k = load_slab(dr["wqkv1"][j, g], 384, NCH)
                            dest_src_holder[0] = lambda bank, tb: bank[:, :].rearrange("p (m r) -> p r m", r=dil)

                            def dst(buf):
                                b3 = buf[:, :].rearrange("p (r m) -> p r m", r=dil)
                                return lambda tb: b3[:, :, tb * mm:(tb + 1) * mm]
                            project_fm(v, k, 0, dst(qT), lambda tb: [("q",)], scale=0.125)
                            project_fm(v, k, P, dst(kT), lambda tb: [("k",)])
                            project_fm(v, k, 2 * P, dst(vT), lambda tb: [("vT",)])
                            transpose_v(lambda t: vT[:, t * P:(t + 1) * P], 16, vaug, lambda t: ("vaug", t), lambda t: {("vT",)})
                            for hh in range(2):
                                hs = slice(hh * 64, (hh + 1) * 64)
                                s_ctr = [0]
                                o_ctr = [0]
                                SB = (2, 3, 6)

                                def emit_S(r, kt):
                                    base = r * L
                                    b_lo, b_hi = max(kt - 1, 0), min(kt + 1, nkt - 1)
                                    n = (b_hi - b_lo + 1) * P
                                    sbi = SB[s_ctr[0] % 3]
                                    pti = s_ctr[0] % 4
                                    s_ctr[0] += 1
                                    c0 = (b_lo - kt + 1) * P
                                    S.op("pe", lambda: nc.tensor.matmul(banks[sbi][:, 0:n], lhsT=sid4[:, hh, g, :], rhs=dbase[:, c0:c0 + n],
                                                                           start=True, stop=False),
                                         reads=[("sid", bb), ("dbase",)], writes=[("ps", sbi)], sig=False)
                                    S.op("pe", lambda: nc.tensor.matmul(banks[sbi][:, 0:n], lhsT=kT[hs, base + kt * P: base + (kt + 1) * P],
                                                                           rhs=qT[hs, base + b_lo * P: base + b_lo * P + n], start=False, stop=True),
                                         reads=[("k",), ("q",)], writes=[("ps", sbi)], sig=True)
                                    S.op("act", lambda: nc.scalar.activation(out=PT[:, pti, 0:n], in_=banks[sbi][:, 0:n], func=AF.Exp),
                                         reads=[("ps", sbi)], writes=[("PT", pti)])
                                    return (pti, b_lo)

                                def flush_O(ob, n4):
                                    if dil == 1:
                                        dst_ap = acc[:, hh, n4 * TB:(n4 + 1) * TB]
                                        src_ap = banks[ob][:, :]
                                    elif dil == 4:
                                        dst_ap = acc[:, hh, :].rearrange("p (m r) -> p r m", r=4)[:, n4, :]
                                        src_ap = banks[ob][:, :]
                                    else:
                                        dst_ap = acc[:, hh, :].rearrange("p (q r) -> p r q", r=16)[:, 4 * n4:4 * n4 + 4, :]
                                        src_ap = banks[ob][:, :].rearrange("p (r q) -> p r q", r=4)
                                    if g == 0:
                                        S.op("dve", lambda: nc.vector.tensor_copy(out=dst_ap, in_=src_ap),
                                             reads=[("ps", ob)], writes=[("acc", hh)])
                                    else:
                                        S.op("dve", lambda: nc.vector.tensor_tensor(out=dst_ap, in0=dst_ap, in1=src_ap, op=ALU.add),
                                             reads=[("ps", ob), ("acc", hh)], writes=[("acc", hh)])

                                def emit_PV(r, b, pts):
                                    gb = o_ctr[0]
                                    ob = 4 + (gb // 4) % 2
                                    col = (gb % 4) * P
                                    kts = [kt for kt in (b - 1, b, b + 1) if 0 <= kt < nkt]
                                    for n_, kt in enumerate(kts):
                                        pti, b_lo = pts[(r, kt)]
                                        S.op("pe", lambda kt=kt, pti=pti, b_lo=b_lo, n_=n_: nc.tensor.matmul(
                                            banks[ob][:, col:col + P],
                                            lhsT=vaug[:, r * nkt + kt, hh:hh + 2, :].rearrange("p a d -> p (a d)"),
                                            rhs=PT[:, pti, (b - b_lo) * P:(b - b_lo + 1) * P],
                                            start=n_ == 0, stop=n_ == len(kts) - 1),
                                            reads=[("vaug", r * nkt + kt), ("PT", pti)], writes=[("ps", ob)], sig=n_ == len(kts) - 1)
                                    o_ctr[0] += 1
                                    if o_ctr[0] % 4 == 0:
                                        flush_O(ob, o_ctr[0] // 4 - 1)

                                tiles = [(r, kt) for r in range(dil) for kt in range(nkt)]
                                LA = 2
                                pts = {}
                                for i in range(min(LA, len(tiles))):
                                    pts[tiles[i]] = emit_S(*tiles[i])
                                for i, (r, b) in enumerate(tiles):
                                    if i + LA < len(tiles):
                                        pts[tiles[i + LA]] = emit_S(*tiles[i + LA])
                                    emit_PV(r, b, pts)
                        for tb in range(NTB):
                            normalize_pair(acc[:, 0, tbs(tb)], acc[:, 1, tbs(tb)], [("acc", 0)], [("acc", 1)], aoT[:, j, tbs(tb)],
                                           [("ao", j, tb), ("ao", j, tb)], Rb, "Rb")
                    S.barrier()
                except _Stop:
                    stopped = True
                inner.close()
                if not stopped:
                    try:
                        oproj_phase(layer, dr["wo1"], aoT)
                    except _Stop:
                        stopped = True
            if stopped:
                raise _Stop()

        try:
            for layer in layers:
                if layer % 2 == 0:
                    na_phase(layer)
                else:
                    dil_phase(layer)
                chk("mixer")
                ffn_phase(layer)
        except _Stop:
            pass

        for c in range(NCH):
            S.dma("sp", "ostore", lambda c=c: nc.sync.dma_start(out=outT[c * P:(c + 1) * P, :], in_=xT[:, c, :]),
                  reads=[("x", c, tb) for tb in range(NTB)])
        S.wait_all("sp")
    return nc


_CACHE = {}
STOP = None


class _Stop(Exception):
    pass


def chk(name):
    if STOP == name:
        raise _Stop()


def _get_nc(layers):
    key = tuple(layers)
    if key not in _CACHE:
        _CACHE[key] = build(list(layers))
    return _CACHE[key]


FUSED = True


def _run(layers, xT_list, inputs):
    nc = _get_nc(layers)
    shared = _host_layout(inputs, layers)
    in_maps = [dict(shared, xT=xT_list[b]) for b in range(8)]
    res = run_bass_kernel_spmd(nc, in_maps, core_ids=list(range(8)))
    return [np.asarray(r["outT"]) for r in res.results]


def kernel(**inputs):
    inputs = {k: np.asarray(v) for k, v in inputs.items()}
    x = inputs["x"]
    xT = [np.ascontiguousarray(x[b].T) for b in range(8)]
    if FUSED:
        outT = _run((0, 1), xT, inputs)
    else:
        mid = _run((0,), xT, inputs)
        outT = _run((1,), mid, inputs)
    return np.ascontiguousarray(np.stack([o.T for o in outT], axis=0)).astype(np.float32)
```

```python
import numpy as np
from contextlib import ExitStack
import concourse.bass as bass
import concourse.mybir as mybir
from concourse.bass_utils import run_bass_kernel_spmd

F32 = mybir.dt.float32
BF16 = mybir.dt.bfloat16
AF = mybir.ActivationFunctionType
ALU = mybir.AluOpType

P = 128
D = 1024
NCH = 8
NT = 2048
TB = 512
NTB = 4
DFF = 2816
NFC = 22
NH = 16
GRID_W = 64
NROWS = 32
DIL = (1, 4, 16)
NEG = -30000.0
EPS = 1e-6
NSLOT = 3
SLOT_ELEMS = 3072


class Sched:
    def __init__(self, nc, es):
        self.nc = nc
        self.es = es
        self.h = {"pe": nc.tensor, "act": nc.scalar, "dve": nc.vector, "pool": nc.gpsimd, "sp": nc.sync}
        self.sem = {}
        self.count = {}
        self.pending = {}
        for e in self.h:
            self.sem[e] = es.enter_context(nc.semaphore("sem_" + e))
            self.count[e] = 0
            self.pending[e] = False
        self.waited = {e: {} for e in self.h}
        self.lastw = {}
        self.readers = {}
        self.last_barrier = {}

    def dma_sem(self, name):
        if name not in self.sem:
            self.sem[name] = self.es.enter_context(self.nc.semaphore("dsem_" + name))
            self.count[name] = 0
        return name

    def _deps(self, eng, reads, writes):
        deps = {}

        def add(src, tok):
            if deps.get(src, 0) < tok:
                deps[src] = tok

        for k in reads:
            for src, tok in self.lastw.get(k, {}).items():
                add(src, tok)
            if isinstance(k, tuple) and k and k[0] == "ps":
                for src, tok in self.readers.get(k, {}).items():
                    if src != eng:
                        add(src, tok)
        for k in writes:
            for src, tok in self.lastw.get(k, {}).items():
                add(src, tok)
            for src, tok in self.readers.get(k, {}).items():
                add(src, tok)
        return deps

    def _wait(self, eng, deps):
        for src, tok in deps.items():
            if src == eng and eng == "pe":
                continue
            if self.waited[eng].get(src, 0) < tok:
                if src == eng:
                    assert tok <= self.count[eng], (eng, tok, self.count[eng])
                self.h[eng].wait_ge(self.sem[src], tok)
                self.waited[eng][src] = tok

    def _record(self, src, tok, reads, writes):
        for k in reads:
            self.readers.setdefault(k, {})[src] = tok
        for k in writes:
            self.lastw[k] = {src: tok}
            self.readers[k] = {}

    def op(self, eng, fn, reads=(), writes=(), sig=True):
        self._wait(eng, self._deps(eng, reads, writes))
        ins = fn()
        if sig:
            self.count[eng] += 1
            ins.then_inc(self.sem[eng], 1)
            tok = self.count[eng]
            self.pending[eng] = False
        else:
            tok = self.count[eng] + 1
            self.pending[eng] = True
        self._record(eng, tok, reads, writes)
        return tok

    def dma(self, queue, semname, fn, reads=(), writes=(), after_barrier=False):
        self.dma_sem(semname)
        deps = self._deps(semname, reads, writes)
        if after_barrier:
            for e, c in self.last_barrier.items():
                if deps.get(e, 0) < c:
                    deps[e] = c
        self._wait(queue, deps)
        ins = fn()
        self.count[semname] += 16
        ins.then_inc(self.sem[semname], 16)
        self._record(semname, self.count[semname], reads, writes)

    def barrier(self, engines=("pe", "act", "dve")):
        for e in engines:
            assert not self.pending[e], e
        for e in engines:
            for e2 in engines:
                if e2 != e and self.waited[e].get(e2, 0) < self.count[e2]:
                    self.h[e].wait_ge(self.sem[e2], self.count[e2])
                    self.waited[e][e2] = self.count[e2]
        self.last_barrier = {e: self.count[e] for e in engines if self.count[e] > 0}

    def wait_all(self, eng):
        for src, c in self.count.items():
            if c > 0 and src != eng and self.waited[eng].get(src, 0) < c:
                self.h[eng].wait_ge(self.sem[src], c)
                self.waited[eng][src] = c


def _alibi_slopes():
    return (2.0 ** (-8.0 * np.arange(1, NH + 1, dtype=np.float64) / NH)).astype(np.float32)


def _na_bias_table(rpb):
    col = np.arange(GRID_W)
    cs = np.clip(col - 8, 0, GRID_W - 16)
    cmask = (col[None, :] >= cs[:, None]) & (col[None, :] < cs[:, None] + 16)
    cidx = np.clip(col[None, :] - col[:, None] + 15, 0, 30)
    out = np.empty((8, 2, 64, 2, 7, 2, 64), np.float32)
    for s in range(2):
        for par in range(2):
            for m in range(7):
                row = 2 * m + par + s
                g = rpb[:, row][:, cidx]
                g = np.where(cmask[None], g, np.float32(NEG))
                g = g.reshape(8, 2, 64, 64)
                out[:, s, :, par, m, :, :] = g.transpose(0, 3, 1, 2)
    return np.ascontiguousarray(out.reshape(8, 128, 2 * 2 * 7 * 64))


def _dil_base_tile():
    p = np.arange(128)[:, None]
    q = np.arange(384)[None, :] - 128
    d = np.abs(q - p)
    return np.where(d <= 64, -d, NEG).astype(np.float32)


def _dil_scaled_ident():
    sl = _alibi_slopes()
    out = np.zeros((8, 128, 2, 3, 128), np.float32)
    eye = np.eye(128, dtype=np.float32)
    for j in range(8):
        for hh in range(2):
            for g in range(3):
                out[j, :, hh, g, :] = eye * np.float32(sl[2 * j + hh] * DIL[g])
    return np.ascontiguousarray(out.reshape(8, 128, 768))


def _host_layout(inputs, layers):
    f = lambda a: np.ascontiguousarray(a, dtype=np.float32)
    shared = {}
    gains = np.empty((128, 4, 2, NCH), np.float32)
    for kind, nm in enumerate(["norm_mix_pre", "norm_mix_post", "norm_ffn_pre", "norm_ffn_post"]):
        gains[:, kind] = inputs[nm].reshape(2, NCH, 128).transpose(2, 0, 1)
    shared["gains"] = f(gains.reshape(128, 64))
    shared["ident"] = f(np.eye(128, dtype=np.float32))
    if 0 in layers:
        w = inputs["na_w_qkv"][0].reshape(D, 3, 8, 128)
        shared["wqkv0"] = f(w.transpose(2, 0, 1, 3).reshape(8, D, 384))
        shared["wo0"] = f(inputs["na_w_o"][0])
        shared["nabias"] = _na_bias_table(np.asarray(inputs["na_rpb"][0], np.float32))
    if 1 in layers:
        w = inputs["dil_w_qkv"][0].reshape(D, 3, 3, 8, 128)
        shared["wqkv1"] = f(w.transpose(3, 1, 0, 2, 4).reshape(8, 3, D, 384))
        shared["wo1"] = f(inputs["dil_w_o"][0])
        shared["dbase"] = _dil_base_tile()
        shared["dsid"] = _dil_scaled_ident()
    for l in layers:
        g = inputs["ffn_w_gate"][l].reshape(D, NFC, 128)
        u = inputs["ffn_w_up"][l].reshape(D, NFC, 128)
        shared[f"wgu{l}"] = f(np.stack([g, u], axis=2).transpose(1, 0, 2, 3).reshape(NFC, D, 256))
        shared[f"wd{l}"] = f(inputs["ffn_w_down"][l].reshape(DFF, NCH, 128).transpose(1, 0, 2))
    return shared


def build(layers):
    nc = bass.Bass("TRN2", target_bir_lowering=False)
    dr = {}

    def din(name, shape):
        dr[name] = nc.dram_tensor(name, list(shape), F32, kind="ExternalInput").ap()

    din("xT", (D, NT))
    din("gains", (128, 64))
    din("ident", (128, 128))
    if 0 in layers:
        din("wqkv0", (8, D, 384))
        din("wo0", (D, D))
        din("nabias", (8, 128, 1792))
    if 1 in layers:
        din("wqkv1", (8, 3, D, 384))
        din("wo1", (D, D))
        din("dbase", (128, 384))
        din("dsid", (8, 128, 768))
    for l in layers:
        din(f"wgu{l}", (NFC, D, 256))
        din(f"wd{l}", (NCH, DFF, 128))
    outT = nc.dram_tensor("outT", [D, NT], F32, kind="ExternalOutput").ap()

    with ExitStack() as es:
        E = es.enter_context
        S = Sched(nc, es)
        sb = lambda name, shape, dt: E(nc.sbuf_tensor(name, list(shape), dt))

        xT = sb("xT_sb", (P, NCH, NT), F32)
        HH = {}
        gains = sb("gains_sb", (P, 64), F32)
        ones_bf = sb("ones_bf", (P, P), BF16)
        ident_bf = sb("ident_bf", (P, P), BF16)
        epsc = sb("epsc", (P, 1), F32)
        wslots = sb("wslots", (P, NSLOT, SLOT_ELEMS), BF16)
        sqb = sb("sqb", (P, 2, TB), BF16)
        rstd_t = sb("rstd_t", (P, 1, TB), F32)
        rstd = sb("rstd", (P, 2, TB), F32)
        utmp = sb("utmp", (P, 2, TB), F32)
        banks = [E(nc.psum_tensor(f"bank{i}", [P, TB], F32)) for i in range(8)]

        def tbs(tb):
            return slice(tb * TB, (tb + 1) * TB)

        def gain_ap(kind, layer, c):
            i = (kind * 2 + layer) * NCH + c
            return gains[:, i:i + 1]

        for c in range(NCH):
            S.dma("sp", "xload", lambda c=c: nc.sync.dma_start(out=xT[:, c, :], in_=dr["xT"][c * P:(c + 1) * P, :]))
        for c in range(NCH):
            for tb in range(NTB):
                S.lastw[("x", c, tb)] = {"xload": S.count["xload"]}
        S.dma("sp", "cload", lambda: nc.sync.dma_start(out=gains[:], in_=dr["gains"]), writes=[("gains",)])
        S.dma("pool", "cload2", lambda: nc.gpsimd.dma_start(out=ident_bf[:], in_=dr["ident"]), writes=[("ident",)])
        S.op("dve", lambda: nc.vector.memset(ones_bf[:], 1.0), writes=[("ones",)])
        S.op("dve", lambda: nc.vector.memset(epsc[:], EPS), writes=[("eps",)])

        slot_ctr = [0]

        def load_slab(src_ap, ncols, nk):
            s = slot_ctr[0] % NSLOT
            slot_ctr[0] += 1
            view = wslots[:, s, 0:nk * ncols].rearrange("p (k c) -> p k c", c=ncols)
            S.dma("pool", f"wslot{s}",
                  lambda: nc.gpsimd.dma_start(out=view, in_=src_ap.rearrange("(k p) c -> p k c", p=P)),
                  writes=[("wslot", s)])
            return view, ("wslot", s)

        sq_ctr = [0]
        ev_ctr = [0]

        def stat_square(src_ap, src_key):
            i = sq_ctr[0] % 2
            sq_ctr[0] += 1
            S.op("act", lambda: nc.scalar.activation(out=sqb[:, i, :], in_=src_ap, func=AF.Square),
                 reads=[src_key], writes=[("sq", i)])
            return i

        def stat_mm(i, stat_bank, stat_key, first, last):
            S.op("pe", lambda: nc.tensor.matmul(stat_bank[:, :], lhsT=ones_bf[:], rhs=sqb[:, i, :], start=first, stop=last),
                 reads=[("sq", i), ("ones",)], writes=[stat_key], sig=True)

        def stat_accum(src_ap, src_key, stat_bank, stat_key, first, last):
            stat_mm(stat_square(src_ap, src_key), stat_bank, stat_key, first, last)

        def finish_rstd(stat_bank, stat_key, ri):
            S.op("act", lambda: nc.scalar.activation(out=rstd_t[:, 0, :], in_=stat_bank[:, :], func=AF.Sqrt,
                                                     bias=epsc[:, 0:1], scale=1.0 / D),
                 reads=[stat_key, ("eps",)], writes=[("rstd_t", 0)])
            S.op("dve", lambda: nc.vector.reciprocal(out=rstd[:, ri, :], in_=rstd_t[:, 0, :]),
                 reads=[("rstd_t", 0)], writes=[("rstd", ri)])

        def pre_norm(layer, kind, tb_list, stat_bank_ids):
            for n, tb in enumerate(tb_list):
                bi = stat_bank_ids[n % len(stat_bank_ids)]
                bank, bkey = banks[bi], ("ps", bi)
                for c in range(NCH):
                    stat_accum(xT[:, c, tbs(tb)], ("x", c, tb), bank, bkey, c == 0, c == NCH - 1)
                ri = n % 2
                finish_rstd(bank, bkey, ri)
                for c in range(NCH):
                    S.op("dve", lambda c=c: nc.vector.scalar_tensor_tensor(
                        out=HH['h'](c, tb), in0=xT[:, c, tbs(tb)], scalar=gain_ap(kind, layer, c),
                        in1=rstd[:, ri, :], op0=ALU.mult, op1=ALU.mult),
                        reads=[("x", c, tb), ("rstd", ri), ("gains",)], writes=[("h", c, tb)])

        def post_norm_update(layer, kind, tb, ysb, ysb_keyf, ri):
            for c in range(NCH):
                ui = c % 2
                S.op("dve", lambda c=c, ui=ui: nc.vector.scalar_tensor_tensor(
                    out=utmp[:, ui, :], in0=ysb(c), scalar=gain_ap(kind, layer, c), in1=rstd[:, ri, :],
                    op0=ALU.mult, op1=ALU.mult),
                    reads=[ysb_keyf(c), ("rstd", ri), ("gains",)], writes=[("utmp", ui)])
                S.op("dve", lambda c=c, ui=ui: nc.vector.tensor_tensor(
                    out=xT[:, c, tbs(tb)], in0=xT[:, c, tbs(tb)], in1=utmp[:, ui, :], op=ALU.add),
                    reads=[("utmp", ui), ("x", c, tb)], writes=[("x", c, tb)])

        def evac_copy(out_ap, in_ap, reads, writes, scale=None):
            ev_ctr[0] += 1
            if ev_ctr[0] % 2 == 0:
                if scale is None:
                    S.op("act", lambda: nc.scalar.activation(out=out_ap, in_=in_ap, func=AF.Copy), reads=reads, writes=writes)
                else:
                    S.op("act", lambda: nc.scalar.activation(out=out_ap, in_=in_ap, func=AF.Copy, scale=scale), reads=reads, writes=writes)
            else:
                if scale is None:
                    S.op("dve", lambda: nc.vector.tensor_copy(out=out_ap, in_=in_ap), reads=reads, writes=writes)
                else:
                    S.op("dve", lambda: nc.vector.tensor_scalar(out=out_ap, in0=in_ap, scalar1=scale, scalar2=None, op0=ALU.mult),
                         reads=reads, writes=writes)

        def oproj_phase(layer, wo_ap, aoT):
            stopped = False
            with ExitStack() as ps_:
                ysb2 = ps_.enter_context(nc.sbuf_tensor(f"ysb_o{layer}", [P, 2, NCH, TB], F32))
                try:
                    slabs = []
                    for s3 in range(3):
                        ncols = 384 if s3 < 2 else 256
                        v, k = load_slab(wo_ap[:, s3 * 384: s3 * 384 + ncols], ncols, NCH)
                        slabs.append((v, k))
                    chk("oslab")
                    pb = [0]
                    for tb in range(NTB):
                        stat_bank, stat_key = banks[2 + tb % 2], ("ps", 2 + tb % 2)
                        pend = None
                        for dc in range(NCH):
                            v, k = slabs[dc // 3]
                            co = (dc % 3) * P
                            bi = pb[0] % 2
                            pb[0] += 1
                            for kc in range(NCH):
                                S.op("pe", lambda kc=kc: nc.tensor.matmul(banks[bi][:, :], lhsT=v[:, kc, co:co + P],
                                                                             rhs=aoT[:, kc, tbs(tb)], start=kc == 0, stop=kc == NCH - 1),
                                     reads=[k, ("ao", kc, tb)], writes=[("ps", bi)], sig=kc == NCH - 1)
                            if pend is not None:
                                stat_mm(*pend)
                            S.op("dve", lambda dc=dc, bi=bi: nc.vector.tensor_copy(out=ysb2[:, tb % 2, dc, :], in_=banks[bi][:, :]),
                                 reads=[("ps", bi)], writes=[("ysb", tb % 2, dc)])
                            pend = (stat_square(banks[bi][:, :], ("ps", bi)), stat_bank, stat_key, dc == 0, dc == NCH - 1)
                        stat_mm(*pend)
                        chk("omm")
                        finish_rstd(stat_bank, stat_key, tb % 2)
                        chk("orstd")
                        post_norm_update(layer, 1, tb, lambda c, tb=tb: ysb2[:, tb % 2, c, :], lambda c, tb=tb: ("ysb", tb % 2, c), tb % 2)
                        chk("onorm")
                    S.barrier()
                except _Stop:
                    stopped = True
            if stopped:
                raise _Stop()

        def ffn_phase(layer):
            wgu, wd = dr[f"wgu{layer}"], dr[f"wd{layer}"]
            with ExitStack() as ps_:
                actT = ps_.enter_context(nc.sbuf_tensor(f"actT{layer}", [P, NFC, 2 * TB], BF16))
                ysb = ps_.enter_context(nc.sbuf_tensor(f"ysb_f{layer}", [P, NCH, 2 * TB], F32))
                sg = ps_.enter_context(nc.sbuf_tensor(f"sg{layer}", [P, 2, TB], F32))
                hTf = ps_.enter_context(nc.sbuf_tensor(f"hTf{layer}", [P, NCH, 2 * TB], BF16))
                HH['h'] = lambda c, tb: hTf[:, c, (tb % 2) * TB:(tb % 2 + 1) * TB]
                pre_norm(layer, 2, [0, 1], [6, 7])
                for half in range(2):
                    tb_list = [2 * half, 2 * half + 1]
                    n = 0
                    for fc in range(NFC):
                        v, k = load_slab(wgu[fc], 256, NCH)
                        for t2, tb in enumerate(tb_list):
                            bg, bu = n % 2, 2 + n % 2
                            n += 1
                            for kc in range(NCH):
                                S.op("pe", lambda kc=kc: nc.tensor.matmul(banks[bg][:, :], lhsT=v[:, kc, 0:P], rhs=HH['h'](kc, tb),
                                                                             start=kc == 0, stop=kc == NCH - 1),
                                     reads=[k, ("h", kc, tb)], writes=[("ps", bg)], sig=kc == NCH - 1)
                            for kc in range(NCH):
                                S.op("pe", lambda kc=kc: nc.tensor.matmul(banks[bu][:, :], lhsT=v[:, kc, P:2 * P], rhs=HH['h'](kc, tb),
                                                                             start=kc == 0, stop=kc == NCH - 1),
                                     reads=[k, ("h", kc, tb)], writes=[("ps", bu)], sig=kc == NCH - 1)
                            si = n % 2
                            S.op("act", lambda: nc.scalar.activation(out=sg[:, si, :], in_=banks[bg][:, :], func=AF.Silu),
                                 reads=[("ps", bg)], writes=[("sg", si)])
                            S.op("dve", lambda: nc.vector.tensor_tensor(out=actT[:, fc, t2 * TB:(t2 + 1) * TB], in0=sg[:, si, :],
                                                                        in1=banks[bu][:, :], op=ALU.mult),
                                 reads=[("sg", si), ("ps", bu)], writes=[("act", fc, t2)])
                    if half == 0:
                        pre_norm(layer, 2, [2, 3], [6, 7])
                    n = 0
                    pend = None
                    for dc in range(NCH):
                        v, k = load_slab(wd[dc], P, NFC)
                        for t2, tb in enumerate(tb_list):
                            bi = 4 + n % 2
                            n += 1
                            for fc in range(NFC):
                                S.op("pe", lambda fc=fc: nc.tensor.matmul(banks[bi][:, :], lhsT=v[:, fc, :],
                                                                             rhs=actT[:, fc, t2 * TB:(t2 + 1) * TB],
                                                                             start=fc == 0, stop=fc == NFC - 1),
                                     reads=[k, ("act", fc, t2)], writes=[("ps", bi)], sig=fc == NFC - 1)
                            if pend is not None:
                                stat_mm(*pend)
                            S.op("dve", lambda: nc.vector.tensor_copy(out=ysb[:, dc, t2 * TB:(t2 + 1) * TB], in_=banks[bi][:, :]),
                                 reads=[("ps", bi)], writes=[("ysbf", dc, t2)])
                            pend = (stat_square(banks[bi][:, :], ("ps", bi)), banks[6 + t2], ("ps", 6 + t2), dc == 0, dc == NCH - 1)
                    stat_mm(*pend)
                    for t2, tb in enumerate(tb_list):
                        finish_rstd(banks[6 + t2], ("ps", 6 + t2), t2)
                        post_norm_update(layer, 3, tb, lambda c, t2=t2: ysb[:, c, t2 * TB:(t2 + 1) * TB],
                                         lambda c, t2=t2: ("ysbf", c, t2), t2)
                S.barrier()

        def project_fm(v, k, col0, dest_fn, dest_keys_fn, scale=None, evac=None):
            for tb in range(NTB):
                bi = proj_ctr[0] % 2
                proj_ctr[0] += 1
                for kc in range(NCH):
                    S.op("pe", lambda kc=kc: nc.tensor.matmul(banks[bi][:, :], lhsT=v[:, kc, col0:col0 + P], rhs=HH['h'](kc, tb),
                                                                 start=kc == 0, stop=kc == NCH - 1),
                         reads=[k, ("h", kc, tb)], writes=[("ps", bi)], sig=kc == NCH - 1)
                if evac is not None:
                    evac(bi, tb)
                else:
                    evac_copy(dest_fn(tb), dest_src(banks[bi], tb), [("ps", bi)], dest_keys_fn(tb), scale=scale)

        proj_ctr = [0]
        dest_src_holder = [None]

        def dest_src(bank, tb):
            return dest_src_holder[0](bank, tb)

        def transpose_v(vT_fn, ntiles, vaug4, vkey_fn, vT_keys_fn):
            tbank = banks[7].bitcast(BF16)
            t = 0
            while t < ntiles:
                n = min(8, ntiles - t)
                for i in range(n):
                    S.op("pe", lambda i=i: nc.tensor.transpose(tbank[:, i * P:(i + 1) * P], vT_fn(t + i), ident_bf[:]),
                         reads=list(vT_keys_fn(t + i)) + [("ident",)], writes=[("ps", 7)], sig=i == n - 1)
                src = tbank[:, 0:n * P].rearrange("p (n h d) -> p n h d", h=2, d=64)
                S.op("dve", lambda t=t, n=n, src=src: nc.vector.tensor_copy(out=vaug4[:, t:t + n, 0:3:2, :], in_=src),
                     reads=[("ps", 7)], writes=[vkey_fn(t + i) for i in range(n)])
                t += n

        def normalize_pair(A_ap, B_ap, A_keys, B_keys, out_ap, out_keys, Rbuf, Rkey):
            S.op("act", lambda: nc.scalar.activation(out=Rbuf[0:64, :], in_=A_ap[64:128, :], func=AF.Ln),
                 reads=A_keys, writes=[(Rkey, 0)])
            S.op("act", lambda: nc.scalar.activation(out=Rbuf[64:128, :], in_=B_ap[0:64, :], func=AF.Ln),
                 reads=B_keys, writes=[(Rkey, 1)])
            S.op("act", lambda: nc.scalar.activation(out=Rbuf[:, :], in_=Rbuf[:, :], func=AF.Exp, scale=-1.0),
                 reads=[(Rkey, 0), (Rkey, 1)], writes=[(Rkey, 0), (Rkey, 1)])
            S.op("dve", lambda: nc.vector.tensor_tensor(out=out_ap[0:64, :], in0=A_ap[0:64, :], in1=Rbuf[0:64, :], op=ALU.mult),
                 reads=list(A_keys) + [(Rkey, 0)], writes=[out_keys[0]])
            S.op("dve", lambda: nc.vector.tensor_tensor(out=out_ap[64:128, :], in0=B_ap[64:128, :], in1=Rbuf[64:128, :], op=ALU.mult),
                 reads=list(B_keys) + [(Rkey, 1)], writes=[out_keys[1]])

        def na_phase(layer):
            with ExitStack() as ps_:
                A = ps_.enter_context
                aoT = A(nc.sbuf_tensor("aoT0", [P, NCH, NT], BF16))
                hT = A(nc.sbuf_tensor("hT0", [P, NCH, NT], BF16))
                HH['h'] = lambda c, tb: hT[:, c, tbs(tb)]
                inner = ExitStack()
                ps_.callback(inner.close)
                A = inner.enter_context
                stopped = False
                try:
                    qbd = A(nc.sbuf_tensor("qbd0", [P, NROWS, 2, 64], BF16))
                    kT = A(nc.sbuf_tensor("kT0", [P, NT], BF16))
                    vT = A(nc.sbuf_tensor("vT0", [P, NT], BF16))
                    vaug = A(nc.sbuf_tensor("vaug0", [P, 31, 3, 64], BF16))
                    biasb = A(nc.sbuf_tensor("nabias_sb", [P, 2, 1792], BF16))
                    PT = A(nc.sbuf_tensor("PT0", [P, 3, TB], BF16))
                    Rb = A(nc.sbuf_tensor("Rb0", [P, TB], F32))
                    S.op("dve", lambda: nc.vector.memset(vaug[:], 1.0), writes=[("vaug", t) for t in range(31)])
                    S.op("dve", lambda: nc.vector.memset(qbd[:], 0.0), writes=[("q", tb) for tb in range(NTB)] + [("q2", tb) for tb in range(NTB)])
                    pre_norm(layer, 0, list(range(NTB)), [0, 1])
                    chk("prenorm")

                    def q_evac(bi, tb):
                        S.op("act", lambda: nc.scalar.activation(
                            out=qbd[0:64, 8 * tb:8 * tb + 8, 0, :], in_=banks[bi][0:64, :].rearrange("p (r c) -> p r c", c=64),
                            func=AF.Copy, scale=0.125), reads=[("ps", bi)], writes=[("q", tb)])
                        S.op("dve", lambda: nc.vector.tensor_scalar(
                            out=qbd[64:128, 8 * tb:8 * tb + 8, 1, :], in0=banks[bi][64:128, :].rearrange("p (r c) -> p r c", c=64),
                            scalar1=0.125, scalar2=None, op0=ALU.mult), reads=[("ps", bi)], writes=[("q2", tb)])

                    dest_src_holder[0] = lambda bank, tb: bank[:, :]
                    unit_ctr = 0
                    for j in range(8):
                        bb = j % 2
                        S.dma("pool", f"nabias{bb}", lambda: nc.gpsimd.dma_start(out=biasb[:, bb, :], in_=dr["nabias"][j]),
                              writes=[("nabias", bb)], after_barrier=True)
                        v, k = load_slab(dr["wqkv0"][j], 384, NCH)
                        project_fm(v, k, 0, None, None, evac=q_evac)
                        project_fm(v, k, P, lambda tb: kT[:, tbs(tb)], lambda tb: [("k", tb)])
                        project_fm(v, k, 2 * P, lambda tb: vT[:, tbs(tb)], lambda tb: [("vT", tb)])
                        chk("proj")
                        transpose_v(lambda r0: vT[:, 64 * r0:64 * r0 + P], 31, vaug, lambda t: ("vaug", t),
                                    lambda r0: {("vT", (64 * r0) // TB), ("vT", (64 * r0 + P - 1) // TB)})
                        chk("transp")
                        bias5 = biasb[:, bb, :].rearrange("p (a m h c) -> p a m h c", a=2, m=7, h=2)
                        for tb in range(NTB):
                            units = [(il, hh) for il in range(8) for hh in range(2)]
                            Obank = {0: 5, 1: 6} if tb % 2 == 0 else {0: 4, 1: 7}
                            pend = None
                            groups = [units[u:u + 2] for u in range(0, 16, 2)]

                            def emit_S(gi, grp):
                                sbi = 2 + (gi % 2)
                                il = grp[0][0]
                                i = 8 * tb + il
                                rs = min(max(i - 4, 0), NROWS - 8)
                                d0p = rs - i + 7
                                par, m0 = d0p % 2, d0p // 2
                                rhs_b = bias5[:, par, m0:m0 + 4, :, :].rearrange("p m h c -> p (m h c)")
                                S.op("pe", lambda: nc.tensor.matmul(banks[sbi][:, :], lhsT=ident_bf[:], rhs=rhs_b, start=True, stop=False),
                                     reads=[("ident",), ("nabias", bb)], writes=[("ps", sbi)], sig=False)
                                for jj in range(4):
                                    k0 = 64 * (rs + 2 * jj)
                                    S.op("pe", lambda jj=jj, k0=k0: nc.tensor.matmul(
                                        banks[sbi][:, jj * P:(jj + 1) * P], lhsT=kT[:, k0:k0 + P],
                                        rhs=qbd[:, i, :, :].rearrange("p h c -> p (h c)"), start=False, stop=jj == 3),
                                        reads=[("k", k0 // TB), ("k", (k0 + P - 1) // TB), ("q", tb), ("q2", tb)], writes=[("ps", sbi)],
                                        sig=(jj == 3))
                                pti = gi % 3
                                S.op("act", lambda: nc.scalar.activation(out=PT[:, pti, :], in_=banks[sbi][:, :], func=AF.Exp),
                                     reads=[("ps", sbi)], writes=[("PT", pti)])

                            def emit_PV(gi, grp):
                                pti = gi % 3
                                for ui, (il, hh) in enumerate(grp):
                                    i = 8 * tb + il
                                    rs = min(max(i - 4, 0), NROWS - 8)
                                    ob = Obank[hh]
                                    for jj in range(4):
                                        r0 = rs + 2 * jj
                                        last_unit = (il == 7 and jj == 3)
                                        S.op("pe", lambda jj=jj, r0=r0: nc.tensor.matmul(
                                            banks[ob][:, il * 64:(il + 1) * 64],
                                            lhsT=vaug[:, r0, hh:hh + 2, :].rearrange("p a d -> p (a d)"),
                                            rhs=PT[:, pti, jj * P + hh * 64: jj * P + (hh + 1) * 64],
                                            start=jj == 0, stop=jj == 3),
                                            reads=[("vaug", r0), ("PT", pti)], writes=[("ps", ob)], sig=(jj == 3))

                            emit_S(0, groups[0])
                            chk("S0")
                            for gi in range(len(groups)):
                                if gi + 1 < len(groups):
                                    emit_S(gi + 1, groups[gi + 1])
                                emit_PV(gi, groups[gi])
                            chk("PV")
                            normalize_pair(banks[Obank[0]], banks[Obank[1]], [("ps", Obank[0])], [("ps", Obank[1])], aoT[:, j, tbs(tb)],
                                           [("ao", j, tb), ("ao", j, tb)], Rb, "Rb")
                            chk("norm1")
                        chk("pair1")
                    chk("attn")
                    S.barrier()
                except _Stop:
                    stopped = True
                inner.close()
                if not stopped:
                    try:
                        oproj_phase(layer, dr["wo0"], aoT)
                    except _Stop:
                        stopped = True
            if stopped:
                raise _Stop()

        def dil_phase(layer):
            with ExitStack() as ps_:
                A = ps_.enter_context
                aoT = A(nc.sbuf_tensor("aoT1", [P, NCH, NT], BF16))
                hT = A(nc.sbuf_tensor("hT1", [P, NCH, NT], BF16))
                HH['h'] = lambda c, tb: hT[:, c, tbs(tb)]
                inner = ExitStack()
                ps_.callback(inner.close)
                A = inner.enter_context
                stopped = False
                try:
                    qT = A(nc.sbuf_tensor("qT1", [P, NT], BF16))
                    kT = A(nc.sbuf_tensor("kT1", [P, NT], BF16))
                    vT = A(nc.sbuf_tensor("vT1", [P, NT], BF16))
                    vaug = A(nc.sbuf_tensor("vaug1", [P, 16, 3, 64], BF16))
                    sid = A(nc.sbuf_tensor("sid_sb", [P, 2, 768], BF16))
                    dbase = A(nc.sbuf_tensor("dbase_sb", [P, 384], BF16))
                    acc = A(nc.sbuf_tensor("acc1", [P, 2, NT], F32))
                    PT = A(nc.sbuf_tensor("PT1", [P, 4, 384], BF16))
                    Rb = A(nc.sbuf_tensor("Rb1", [P, TB], F32))
                    S.op("dve", lambda: nc.vector.memset(vaug[:], 1.0), writes=[("vaug", t) for t in range(16)])
                    S.dma("pool", "dbase", lambda: nc.gpsimd.dma_start(out=dbase[:], in_=dr["dbase"]), writes=[("dbase",)], after_barrier=True)
                    pre_norm(layer, 0, list(range(NTB)), [0, 1])
                    for j in range(8):
                        bb = j % 2
                        S.dma("pool", f"sid{bb}", lambda: nc.gpsimd.dma_start(out=sid[:, bb, :], in_=dr["dsid"][j]), writes=[("sid", bb)], after_barrier=True)
                        sid4 = sid[:, bb, :].rearrange("p (h g c) -> p h g c", h=2, g=3)
                        for g, dil in enumerate(DIL):
                            L = NT // dil
                            nkt = L // P
                            mm = TB // dil
                            v, k = load_slab(dr["wqkv1"][j, g], 384, NCH)
                            dest_src_holder[0] = lambda bank, tb: bank[:, :].rearrange("p (m r) -> p r m", r=dil)

                            def dst(buf):
                                b3 = buf[:, :].rearrange("p (r m) -> p r m", r=dil)
                                return lambda tb: b3[:, :, tb * mm:(tb + 1) * mm]
                            project_fm(v, k, 0, dst(qT), lambda tb: [("q",)], scale=0.125)
                            project_fm(v, k, P, dst(kT), lambda tb: [("k",)])
                            project_fm(v, k, 2 * P, dst(vT), lambda tb: [("vT",)])
                            transpose_v(lambda t: vT[:, t * P:(t + 1) * P], 16, vaug, lambda t: ("vaug", t), lambda t: {("vT",)})
                            for hh in range(2):
                                hs = slice(hh * 64, (hh + 1) * 64)
                                s_ctr = [0]
                                o_ctr = [0]
                                SB = (2, 3, 6)

                                def emit_S(r, kt):
                                    base = r * L
                                    b_lo, b_hi = max(kt - 1, 0), min(kt + 1, nkt - 1)
                                    n = (b_hi - b_lo + 1) * P
                                    sbi = SB[s_ctr[0] % 3]
                                    pti = s_ctr[0] % 4
                                    s_ctr[0] += 1
                                    c0 = (b_lo - kt + 1) * P
                                    S.op("pe", lambda: nc.tensor.matmul(banks[sbi][:, 0:n], lhsT=sid4[:, hh, g, :], rhs=dbase[:, c0:c0 + n],
                                                                           start=True, stop=False),
                                         reads=[("sid", bb), ("dbase",)], writes=[("ps", sbi)], sig=False)
                                    S.op("pe", lambda: nc.tensor.matmul(banks[sbi][:, 0:n], lhsT=kT[hs, base + kt * P: base + (kt + 1) * P],
                                                                           rhs=qT[hs, base + b_lo * P: base + b_lo * P + n], start=False, stop=True),
                                         reads=[("k",), ("q",)], writes=[("ps", sbi)], sig=True)
                                    S.op("act", lambda: nc.scalar.activation(out=PT[:, pti, 0:n], in_=banks[sbi][:, 0:n], func=AF.Exp),
                                         reads=[("ps", sbi)], writes=[("PT", pti)])
                                    return (pti, b_lo)

                                def flush_O(ob, n4):
                                    if dil == 1:
                                        dst_ap = acc[:, hh, n4 * TB:(n4 + 1) * TB]
                                        src_ap = banks[ob][:, :]
                                    elif dil == 4:
                                        dst_ap = acc[:, hh, :].rearrange("p (m r) -> p r m", r=4)[:, n4, :]
                                        src_ap = banks[ob][:, :]
                                    else:
                                        dst_ap = acc[:, hh, :].rearrange("p (q r) -> p r q", r=16)[:, 4 * n4:4 * n4 + 4, :]
                                        src_ap = banks[ob][:, :].rearrange("p (r q) -> p r q", r=4)
                                    if g == 0:
                                        S.op("dve", lambda: nc.vector.tensor_copy(out=dst_ap, in_=src_ap),
                                             reads=[("ps", ob)], writes=[("acc", hh)])
                                    else:
                                        S.op("dve", lambda: nc.vector.tensor_tensor(out=dst_ap, in0=dst_ap, in1=src_ap, op=ALU.add),
                                             reads=[("ps", ob), ("acc", hh)], writes=[("acc", hh)])

                                def emit_PV(r, b, pts):
                                    gb = o_ctr[0]
                                    ob = 4 + (gb // 4) % 2
                                    col = (gb % 4) * P
                                    kts = [kt for kt in (b - 1, b, b + 1) if 0 <= kt < nkt]
                                    for n_, kt in enumerate(kts):
                                        pti, b_lo = pts[(r, kt)]
                                        S.op("pe", lambda kt=kt, pti=pti, b_lo=b_lo, n_=n_: nc.tensor.matmul(
                                            banks[ob][:, col:col + P],
                                            lhsT=vaug[:, r * nkt + kt, hh:hh + 2, :].rearrange("p a d -> p (a d)"),
                                            rhs=PT[:, pti, (b - b_lo) * P:(b - b_lo + 1) * P],
                                            start=n_ == 0, stop=n_ == len(kts) - 1),
                                            reads=[("vaug", r * nkt + kt), ("PT", pti)], writes=[("ps", ob)], sig=n_ == len(kts) - 1)
                                    o_ctr[0] += 1
                                    if o_ctr[0] % 4 == 0:
                                        flush_O(ob, o_ctr[0] // 4 - 1)

                                tiles = [(r, kt) for r in range(dil) for kt in range(nkt)]
                                LA = 2
                                pts = {}
                                for i in range(min(LA, len(tiles))):
                                    pts[tiles[i]] = emit_S(*tiles[i])
                                for i, (r, b) in enumerate(tiles):
                                    if i + LA < len(tiles):
                                        pts[tiles[i + LA]] = emit_S(*tiles[i + LA])
                                    emit_PV(r, b, pts)
                        for tb in range(NTB):
                            normalize_pair(acc[:, 0, tbs(tb)], acc[:, 1, tbs(tb)], [("acc", 0)], [("acc", 1)], aoT[:, j, tbs(tb)],
                                           [("ao", j, tb), ("ao", j, tb)], Rb, "Rb")
                    S.barrier()
                except _Stop:
                    stopped = True
                inner.close()
                if not stopped:
                    try:
                        oproj_phase(layer, dr["wo1"], aoT)
                    except _Stop:
                        stopped = True
            if stopped:
                raise _Stop()

        try:
            for layer in layers:
                if layer % 2 == 0:
                    na_phase(layer)
                else:
                    dil_phase(layer)
                chk("mixer")
                ffn_phase(layer)
        except _Stop:
            pass

        for c in range(NCH):
            S.dma("sp", "ostore", lambda c=c: nc.sync.dma_start(out=outT[c * P:(c + 1) * P, :], in_=xT[:, c, :]),
                  reads=[("x", c, tb) for tb in range(NTB)])
        S.wait_all("sp")
    return nc


_CACHE = {}
STOP = None


class _Stop(Exception):
    pass


def chk(name):
    if STOP == name:
        raise _Stop()


def _get_nc(layers):
    key = tuple(layers)
    if key not in _CACHE:
        _CACHE[key] = build(list(layers))
    return _CACHE[key]


FUSED = True


def _run(layers, xT_list, inputs):
    nc = _get_nc(layers)
    shared = _host_layout(inputs, layers)
    in_maps = [dict(shared, xT=xT_list[b]) for b in range(8)]
    res = run_bass_kernel_spmd(nc, in_maps, core_ids=list(range(8)))
    return [np.asarray(r["outT"]) for r in res.results]


def kernel(**inputs):
    inputs = {k: np.asarray(v) for k, v in inputs.items()}
    x = inputs["x"]
    xT = [np.ascontiguousarray(x[b].T) for b in range(8)]
    if FUSED:
        outT = _run((0, 1), xT, inputs)
    else:
        mid = _run((0,), xT, inputs)
        outT = _run((1,), mid, inputs)
    return np.ascontiguousarray(np.stack([o.T for o in outT], axis=0)).astype(np.float32)
```
